# Optimizing a Trainium2 kernel written in Bass

```python
import math
import jax, jax.numpy as jnp
from jax import lax
import numpy as np

D_MODEL = 1024
BATCH = 2
SEQ = 16384
DEPTH = 1
DEC_BATCH = 8
DEC_SEQ = 4096
PAST_LEN = 128

H_D = 4
DH = 64
DV_D = 2 * DH
W_D = H_D * DV_D
H_G = 4
DK_G = 64
DV_G = 128
W_G = H_G * DV_G
GATE_RANK = 16
GATE_TAU = 16.0
CHUNK = 64
Q_BLOCK = 128
D_FF = 4 * D_MODEL
EPS = 1e-5
ALPHA = (2.0 * DEPTH) ** 0.25
BETA = (8.0 * DEPTH) ** -0.25

O_DQ = 0
O_DK = O_DQ + H_D * 2 * DH
O_DV = O_DK + H_D * 2 * DH
O_GQ = O_DV + W_D
O_GK = O_GQ + H_G * DK_G
O_GV = O_GK + H_G * DK_G
O_GR = O_GV + W_G
O_LRF = O_GR + W_G
O_LRB = O_LRF + GATE_RANK
D_IN = O_LRB + GATE_RANK

kernel_name = "hybrid_diffattn_gla_encoder"


def layer_norm(x, g, b):
    xf = x.astype(jnp.float32)
    mu = jnp.mean(xf, axis=-1, keepdims=True)
    xc = xf - mu
    var = jnp.mean(xc * xc, axis=-1, keepdims=True)
    return (xc * lax.rsqrt(var + EPS) * g.astype(jnp.float32) + b.astype(jnp.float32)).astype(x.dtype)


def rms_norm(x, g):
    xf = x.astype(jnp.float32)
    return xf * lax.rsqrt(jnp.mean(xf * xf, axis=-1, keepdims=True) + EPS) * g.astype(jnp.float32)


def alibi_slopes(n_heads):
    return jnp.asarray(np.array([2.0 ** (-8.0 * (h + 1) / n_heads) for h in range(n_heads)], dtype=np.float32))


def diff_attention(q, k, v, lam):
    B, S = q.shape[0], q.shape[1]
    nblk = S // Q_BLOCK
    slopes = alibi_slopes(H_D)
    scale = DH ** -0.5
    qb = q.reshape(B, nblk, Q_BLOCK, H_D, 2, DH).transpose(1, 0, 2, 3, 4, 5)
    starts = jnp.arange(nblk, dtype=jnp.int32) * Q_BLOCK
    pos_k = jnp.arange(S, dtype=jnp.int32)

    def block(args):
        q_blk, start = args
        s = jnp.einsum('bqhcd,bkhcd->bhcqk', q_blk, k) * scale
        pos_q = start + jnp.arange(Q_BLOCK, dtype=jnp.int32)
        dist = jnp.abs(pos_q[:, None] - pos_k[None, :]).astype(jnp.float32)
        s = s - slopes[None, :, None, None, None] * dist[None, None, None]
        p = jax.nn.softmax(s, axis=-1)
        w = p[:, :, 0] - lam * p[:, :, 1]
        return jnp.einsum('bhqk,bkhe->bqhe', w, v)

    out = lax.map(block, (qb, starts))
    return out.transpose(1, 0, 2, 3, 4).reshape(B, S, H_D, DV_D)


def gla_direction(q, k, v, log_a):
    B, S, H, dk = q.shape
    dv = v.shape[-1]
    n = S // CHUNK

    def to_chunks(t):
        return t.reshape(B, n, CHUNK, H, t.shape[-1]).transpose(1, 0, 3, 2, 4)

    qc, kc, vc, ac = to_chunks(q), to_chunks(k), to_chunks(v), to_chunks(log_a)
    bc = jnp.cumsum(ac, axis=3)
    mask = jnp.tril(jnp.ones((CHUNK, CHUNK), dtype=bool))

    def step(state, inp):
        q_, k_, v_, b_ = inp
        inter = jnp.einsum('bhcd,bhde->bhce', q_ * jnp.exp(b_), state)
        diff = b_[:, :, :, None, :] - b_[:, :, None, :, :]
        decay = jnp.exp(jnp.where(mask[None, None, :, :, None], diff, -jnp.inf))
        attn = jnp.einsum('bhijd,bhjd->bhij', q_[:, :, :, None, :] * decay, k_)
        intra = jnp.einsum('bhij,bhje->bhie', attn, v_)
        b_last = b_[:, :, -1:, :]
        new_state = jnp.exp(b_last[:, :, 0, :])[..., None] * state + jnp.einsum(
            'bhjd,bhje->bhde', k_ * jnp.exp(b_last - b_), v_)
        return new_state, inter + intra

    s0 = jnp.zeros((B, H, dk, dv), dtype=jnp.float32)
    _, out = lax.scan(step, s0, (qc, kc, vc, bc))
    return out.transpose(1, 0, 3, 2, 4).reshape(B, S, H, dv)


def encoder_layer(x, layer_idx, w_in, w_o, lam_q1, lam_k1, lam_q2, lam_k2, diff_norm_g,
                  gla_wa2_f, gla_ba_f, gla_wa2_b, gla_ba_b, gla_norm_g,
                  ln1_g, ln1_b, w_ff1, w_ff2, ln2_g, ln2_b):
    B, S, _ = x.shape
    f32 = jnp.float32
    h = (x @ w_in).astype(f32)

    dq = h[..., O_DQ:O_DK].reshape(B, S, H_D, 2, DH)
    dk = h[..., O_DK:O_DV].reshape(B, S, H_D, 2, DH)
    dvv = h[..., O_DV:O_GQ].reshape(B, S, H_D, DV_D)
    lam_init = 0.8 - 0.6 * math.exp(-0.3 * layer_idx)
    lam = (jnp.exp(jnp.sum(lam_q1.astype(f32) * lam_k1.astype(f32)))
           - jnp.exp(jnp.sum(lam_q2.astype(f32) * lam_k2.astype(f32))) + lam_init)
    od = diff_attention(dq, dk, dvv, lam)
    od = rms_norm(od, diff_norm_g) * (1.0 - lam_init)

    gq = h[..., O_GQ:O_GK].reshape(B, S, H_G, DK_G) * (DK_G ** -0.5)
    gk = h[..., O_GK:O_GV].reshape(B, S, H_G, DK_G)
    gv = h[..., O_GV:O_GR].reshape(B, S, H_G, DV_G)
    gr = h[..., O_GR:O_LRF]
    z_f = h[..., O_LRF:O_LRB] @ gla_wa2_f.astype(f32) + gla_ba_f.astype(f32)
    z_b = h[..., O_LRB:D_IN] @ gla_wa2_b.astype(f32) + gla_ba_b.astype(f32)
    la_f = (jax.nn.log_sigmoid(z_f) / GATE_TAU).reshape(B, S, H_G, DK_G)
    la_b = (jax.nn.log_sigmoid(z_b) / GATE_TAU).reshape(B, S, H_G, DK_G)
    o_f = gla_direction(gq, gk, gv, la_f)
    flip = lambda t: jnp.flip(t, axis=1)
    o_b = flip(gla_direction(flip(gq), flip(gk), flip(gv), flip(la_b)))
    og = rms_norm(o_f + o_b, gla_norm_g).reshape(B, S, W_G)
    og = jax.nn.silu(gr) * og

    mix = jnp.concatenate([od.reshape(B, S, W_D), og], axis=-1).astype(x.dtype) @ w_o
    x = layer_norm(ALPHA * x + mix, ln1_g, ln1_b)

    hid = jnp.square(jax.nn.relu(x @ w_ff1))
    x = layer_norm(ALPHA * x + hid @ w_ff2, ln2_g, ln2_b)
    return x


def setup_inputs(seed: int = 0) -> dict:
    key = jax.random.key(seed)
    ks = jax.random.split(key, 20)
    f32 = jnp.float32
    nrm = lambda k, shape, s: jax.random.normal(k, shape, dtype=f32) * s
    x_prompt = jax.random.normal(ks[0], (BATCH, SEQ, D_MODEL), dtype=f32)
    x_sample = jax.random.normal(ks[1], (DEC_BATCH, DEC_SEQ, D_MODEL), dtype=f32)
    w_in = nrm(ks[2], (DEPTH, D_MODEL, D_IN), D_MODEL ** -0.5)
    col_scale = np.ones((D_IN,), dtype=np.float32)
    col_scale[O_DV:O_GQ] = BETA
    col_scale[O_GV:O_GR] = BETA
    w_in = w_in * jnp.asarray(col_scale)
    w_o = nrm(ks[3], (DEPTH, W_D + W_G, D_MODEL), (W_D + W_G) ** -0.5 * BETA)
    lam_q1 = nrm(ks[4], (DEPTH, DH), 0.1)
    lam_k1 = nrm(ks[5], (DEPTH, DH), 0.1)
    lam_q2 = nrm(ks[6], (DEPTH, DH), 0.1)
    lam_k2 = nrm(ks[7], (DEPTH, DH), 0.1)
    diff_norm_g = 1.0 + nrm(ks[8], (DEPTH, DV_D), 0.02)
    gla_wa2_f = nrm(ks[9], (DEPTH, GATE_RANK, H_G * DK_G), GATE_RANK ** -0.5)
    gla_ba_f = nrm(ks[10], (DEPTH, H_G * DK_G), 0.01)
    gla_wa2_b = nrm(ks[11], (DEPTH, GATE_RANK, H_G * DK_G), GATE_RANK ** -0.5)
    gla_ba_b = nrm(ks[12], (DEPTH, H_G * DK_G), 0.01)
    gla_norm_g = 1.0 + nrm(ks[13], (DEPTH, DV_G), 0.02)
    ln1_g = 1.0 + nrm(ks[14], (DEPTH, D_MODEL), 0.02)
    ln1_b = nrm(ks[15], (DEPTH, D_MODEL), 0.02)
    w_ff1 = nrm(ks[16], (DEPTH, D_MODEL, D_FF), D_MODEL ** -0.5 * BETA)
    w_ff2 = nrm(ks[17], (DEPTH, D_FF, D_MODEL), D_FF ** -0.5 * BETA)
    ln2_g = 1.0 + nrm(ks[18], (DEPTH, D_MODEL), 0.02)
    ln2_b = nrm(ks[19], (DEPTH, D_MODEL), 0.02)
    return {"x_prompt": x_prompt, "x_sample": x_sample, "w_in": w_in, "w_o": w_o,
            "lam_q1": lam_q1, "lam_k1": lam_k1, "lam_q2": lam_q2, "lam_k2": lam_k2,
            "diff_norm_g": diff_norm_g, "gla_wa2_f": gla_wa2_f, "gla_ba_f": gla_ba_f,
            "gla_wa2_b": gla_wa2_b, "gla_ba_b": gla_ba_b, "gla_norm_g": gla_norm_g,
            "ln1_g": ln1_g, "ln1_b": ln1_b, "w_ff1": w_ff1, "w_ff2": w_ff2,
            "ln2_g": ln2_g, "ln2_b": ln2_b}


def reference(x_prompt, x_sample, w_in, w_o, lam_q1, lam_k1, lam_q2, lam_k2, diff_norm_g,
              gla_wa2_f, gla_ba_f, gla_wa2_b, gla_ba_b, gla_norm_g,
              ln1_g, ln1_b, w_ff1, w_ff2, ln2_g, ln2_b):
    def run(x):
        for l in range(DEPTH):
            x = encoder_layer(x, l, w_in[l], w_o[l], lam_q1[l], lam_k1[l], lam_q2[l], lam_k2[l],
                              diff_norm_g[l], gla_wa2_f[l], gla_ba_f[l], gla_wa2_b[l], gla_ba_b[l],
                              gla_norm_g[l], ln1_g[l], ln1_b[l], w_ff1[l], w_ff2[l], ln2_g[l], ln2_b[l])
        return x

    y_prompt = run(x_prompt)
    y_sample = run(x_sample)
    return (y_prompt, y_sample)
```

```python
import math
from contextlib import ExitStack

import numpy as np
import ml_dtypes
import concourse.bass as bass
import concourse.mybir as mybir
from concourse.bass_utils import run_bass_kernel_spmd

F32 = mybir.dt.float32
BF16 = mybir.dt.bfloat16
AF = mybir.ActivationFunctionType
ALU = mybir.AluOpType
AX = mybir.AxisListType

D_MODEL = 1024
D_IN = 3104
D_FF = 4096
O_DQ, O_DK, O_DV, O_GQ, O_GK, O_GV, O_GR, O_LRF, O_LRB = 0, 512, 1024, 1536, 1792, 2048, 2560, 3072, 3088
EPS = 1e-5
ALPHA = 2.0 ** 0.25
LAM_INIT = 0.8 - 0.6 * math.exp(0.0)
SLOPES = [2.0 ** (-8.0 * (h + 1) / 4) for h in range(4)]
FULL_L = 4096

SEM_CAP = 16000
ENGS = ("pe", "act", "dve", "pool", "sp")


class Buf:
    __slots__ = ("name", "t", "last_w", "readers", "dma_sem", "dma_cnt")

    def __init__(self, name, t=None):
        self.name = name
        self.t = t
        self.last_w = None
        self.readers = []
        self.dma_sem = None
        self.dma_cnt = 0

    def __getitem__(self, idx):
        return self.t[idx]


class Op:
    __slots__ = ("eng", "fn", "deps", "signal", "mile", "dma_buf", "dma_target", "is_dma")

    def __init__(self, eng, fn):
        self.eng = eng
        self.fn = fn
        self.deps = []
        self.signal = False
        self.mile = None
        self.is_dma = False
        self.dma_buf = None
        self.dma_target = None


class Sched:
    def __init__(self, nc):
        self.nc = nc
        self.ops = {e: [] for e in ENGS}
        self.bufs = []

    def buf(self, name, t=None):
        b = Buf(name, t)
        self.bufs.append(b)
        return b

    def barrier(self):
        lasts = []
        for e in ENGS:
            for o in reversed(self.ops[e]):
                if o.fn is not None and not o.is_dma:
                    lasts.append(o)
                    break
        dmas = {}
        for e in ENGS:
            for o in self.ops[e]:
                if o.is_dma:
                    dmas[id(o.dma_buf)] = o
        for e in ENGS:
            o = Op(e, None)
            for d in lasts:
                o.deps.append(d)
                d.signal = True
            o.deps.extend(dmas.values())
            self.ops[e].append(o)
        for b in self.bufs:
            b.last_w = None
            b.readers = []

    def op(self, eng, fn, reads=(), writes=(), dma_buf=None, nowaw=False):
        o = Op(eng, fn)
        deps = []
        for b in reads:
            if b.last_w is not None:
                deps.append(b.last_w)
        for b in writes:
            if b.last_w is not None and not (nowaw and b.last_w.is_dma and not b.readers):
                deps.append(b.last_w)
            last_by_eng = {}
            for r in b.readers:
                if r.is_dma or r.eng == "pool":
                    deps.append(r)
                else:
                    last_by_eng[r.eng] = r
            deps.extend(last_by_eng.values())
        if dma_buf is not None:
            o.is_dma = True
            o.dma_buf = dma_buf
            dma_buf.dma_cnt += 1
            o.dma_target = 16 * dma_buf.dma_cnt
        seen = set()
        for d in deps:
            if d is o or id(d) in seen:
                continue
            seen.add(id(d))
            if eng == "pe" and d.eng == "pe" and not d.is_dma:
                continue
            o.deps.append(d)
            if not d.is_dma:
                d.signal = True
        for b in reads:
            b.readers.append(o)
        for b in writes:
            b.last_w = o
            b.readers = []
        self.ops[eng].append(o)
        return o

    def emit(self, newsem):
        nc = self.nc
        eng_sems = {e: [] for e in ENGS}
        for e in ENGS:
            cnt = 0
            for o in self.ops[e]:
                if o.signal and not o.is_dma:
                    o.mile = cnt
                    cnt += 1
            for i in range((cnt + SEM_CAP - 1) // SEM_CAP):
                eng_sems[e].append(newsem("s_%s_%d" % (e, i)))
        for e in ENGS:
            for o in self.ops[e]:
                if o.is_dma and o.dma_buf.dma_sem is None:
                    o.dma_buf.dma_sem = newsem("d_" + o.dma_buf.name)

        def run_engine(e, eng):
            waited = {}
            for o in self.ops[e]:
                for d in o.deps:
                    if d.is_dma:
                        key = ("dma", id(d.dma_buf))
                        if waited.get(key, 0) >= d.dma_target:
                            continue
                        waited[key] = d.dma_target
                        eng.wait_ge(d.dma_buf.dma_sem, d.dma_target)
                    else:
                        key = ("eng", d.eng)
                        if waited.get(key, -1) >= d.mile:
                            continue
                        waited[key] = d.mile
                        eng.wait_ge(eng_sems[d.eng][d.mile // SEM_CAP], d.mile % SEM_CAP + 1)
                if o.fn is None:
                    continue
                ins = o.fn(eng)
                if o.is_dma:
                    ins.then_inc(o.dma_buf.dma_sem, 16)
                elif o.signal:
                    ins.then_inc(eng_sems[e][o.mile // SEM_CAP], 1)
            if e in ("sp", "pool"):
                for o in self.ops[e]:
                    if o.is_dma:
                        key = ("dma", id(o.dma_buf))
                        val = 16 * o.dma_buf.dma_cnt
                        if waited.get(key, 0) >= val:
                            continue
                        waited[key] = val
                        eng.wait_ge(o.dma_buf.dma_sem, val)

        with nc.Block() as block:
            @block.tensor
            def _(eng):
                run_engine("pe", eng)

            @block.scalar
            def _(eng):
                run_engine("act", eng)

            @block.vector
            def _(eng):
                run_engine("dve", eng)

            @block.gpsimd
            def _(eng):
                run_engine("pool", eng)

            @block.sync
            def _(eng):
                run_engine("sp", eng)


class Ring:
    def __init__(self, bufs):
        self.bufs = bufs
        self.i = 0

    def next(self):
        b = self.bufs[self.i % len(self.bufs)]
        self.i += 1
        return b


def build(L):
    NT = L // 512
    NKO = L // 128
    nc = bass.Bass("TRN2", target_bir_lowering=False)

    def din(name, shape, dt=F32):
        return nc.dram_tensor(name, list(shape), dt, kind="ExternalInput").ap()

    def dscr(name, shape, dt):
        return nc.dram_tensor(name, list(shape), dt).ap()

    xT = {"p": din("xpT", [D_MODEL, L]), "s": din("xsT", [D_MODEL, L])}
    xr = {"p": din("xp", [L, D_MODEL]), "s": din("xs", [L, D_MODEL])}
    xoT = din("xoT", [3, D_MODEL, L])
    w_in = din("w_in", [D_MODEL, D_IN])
    wlr_d = din("wlr", [5, D_MODEL, 16])
    wa2_d = din("wa2aug", [17, 5, 256])
    w_o = din("w_o", [D_MODEL, D_MODEL])
    w_ff1 = din("w_ff1", [D_MODEL, D_FF])
    w_ff2 = din("w_ff2", [D_FF, D_MODEL])
    lamv_d = din("lamv", [128, 4, 64])
    dng_d = din("dng", [128, 1])
    gng_d = din("gng", [128, 1])
    lngb_d = din("lngb", [128, 4, D_MODEL])
    tri_d = din("tri", [128, 6, 128])
    ident_d = din("ident", [128, 128])
    dtab_d = din("dtab", [128, 4, 512])
    biasBA_d = din("biasBA", [128, 4, 2, NKO + 1])
    biasO_d = din("biasO", [128, 4, NT, 3 * NKO])
    kaug_d = {"p": din("kaugp", [2, 4 * L], BF16), "s": din("kaugs", [2, L], BF16)}
    qaug_d = din("qaug", [2, 4, 2, 512], BF16)
    flags_d = din("flags", [128, 9])
    yout = {"p": nc.dram_tensor("yp", [L, D_MODEL], F32, kind="ExternalOutput").ap(),
            "s": nc.dram_tensor("ys", [L, D_MODEL], F32, kind="ExternalOutput").ap()}

    NK = {"p": 4 * L, "s": L}
    OWN0 = {"p": 3 * L, "s": 0}
    Kscr = {j: dscr("Kscr" + j, [4, 2, 64, NK[j]], BF16) for j in "ps"}
    Vscr = {j: dscr("Vscr" + j, [NK[j] // 128, 128, 512], BF16) for j in "ps"}
    Qscr = {j: dscr("Qscr" + j, [4, 2, 64, L], BF16) for j in "ps"}
    OFscr = {j: dscr("OFscr" + j, [4, 128, L], F32) for j in "ps"}
    CATscr = {j: dscr("CATscr" + j, [D_MODEL, L], BF16) for j in "ps"}
    X1scr = {j: dscr("X1scr" + j, [L, D_MODEL], F32) for j in "ps"}
    X1Tscr = {j: dscr("X1Tscr" + j, [D_MODEL, L], BF16) for j in "ps"}

    S = Sched(nc)
    uid = [0]

    with ExitStack() as top:
        def newsem(name):
            uid[0] += 1
            return top.enter_context(nc.semaphore("%s_%d" % (name, uid[0])))

        def sbt(es, name, shape, dt):
            uid[0] += 1
            return S.buf(name, es.enter_context(nc.sbuf_tensor("%s_%d" % (name, uid[0]), list(shape), dt)))

        def pst(es, name, shape, dt=F32):
            uid[0] += 1
            return S.buf(name, es.enter_context(nc.psum_tensor("%s_%d" % (name, uid[0]), list(shape), dt)))

        def ring(es, name, n, shape, dt):
            return Ring([sbt(es, "%s%d" % (name, i), shape, dt) for i in range(n)])

        def dma(q, out_ap, in_ap, reads, writes, dbuf, nowaw=False):
            S.op(q, lambda e: e.dma_start(out=out_ap, in_=in_ap), reads=reads, writes=writes, dma_buf=dbuf, nowaw=nowaw)

        def mm(out_b, out_ap, l_b, l_ap, r_b, r_ap, start, stop):
            S.op("pe", lambda e: e.matmul(out_ap, lhsT=l_ap, rhs=r_ap, start=start, stop=stop),
                 reads=[l_b, r_b], writes=[out_b])

        def act(out_b, out_ap, in_b, in_ap, func, bias=None, scale=1.0, extra_reads=()):
            if bias is None:
                S.op("act", lambda e: e.activation(out=out_ap, in_=in_ap, func=func, scale=scale),
                     reads=[in_b] + list(extra_reads), writes=[out_b])
            else:
                S.op("act", lambda e: e.activation(out=out_ap, in_=in_ap, func=func, bias=bias, scale=scale),
                     reads=[in_b] + list(extra_reads), writes=[out_b])

        def tt(eng, out_b, out_ap, a_b, a_ap, b_b, b_ap, op):
            S.op(eng, lambda e: e.tensor_tensor(out=out_ap, in0=a_ap, in1=b_ap, op=op),
                 reads=[a_b, b_b], writes=[out_b])

        def ts(eng, out_b, out_ap, a_b, a_ap, s1, s2, op0, op1=None, extra_reads=()):
            if op1 is None:
                S.op(eng, lambda e: e.tensor_scalar(out=out_ap, in0=a_ap, scalar1=s1, scalar2=None, op0=op0),
                     reads=[a_b] + list(extra_reads), writes=[out_b])
            else:
                S.op(eng, lambda e: e.tensor_scalar(out=out_ap, in0=a_ap, scalar1=s1, scalar2=s2, op0=op0, op1=op1),
                     reads=[a_b] + list(extra_reads), writes=[out_b])

        def stt(eng, out_b, out_ap, a_b, a_ap, sc, b_b, b_ap, op0, op1, extra_reads=()):
            S.op(eng, lambda e: e.scalar_tensor_tensor(out=out_ap, in0=a_ap, scalar=sc, in1=b_ap, op0=op0, op1=op1),
                 reads=[a_b, b_b] + list(extra_reads), writes=[out_b])

        def cp(eng, out_b, out_ap, in_b, in_ap):
            if eng == "act":
                S.op("act", lambda e: e.copy(out=out_ap, in_=in_ap), reads=[in_b], writes=[out_b])
            else:
                S.op(eng, lambda e: e.tensor_copy(out=out_ap, in_=in_ap), reads=[in_b], writes=[out_b])

        def memset(eng, b, ap, val):
            S.op(eng, lambda e: e.memset(ap, val), writes=[b])

        tri = sbt(top, "tri", [128, 6, 128], F32)
        identb = sbt(top, "identb", [128, 128], BF16)
        onesb = sbt(top, "onesb", [128, 128], BF16)
        onesf = sbt(top, "onesf", [128, 128], F32)
        negsix = sbt(top, "negsix", [128, 1], F32)
        lamv = sbt(top, "lamv", [128, 4, 64], F32)
        lamt = sbt(top, "lamt", [128, 64], F32)
        lams = sbt(top, "lams", [128, 4], F32)
        neglam = sbt(top, "neglam", [128, 1], F32)
        dgs = sbt(top, "dgs", [128, 1], F32)
        gng = sbt(top, "gng", [128, 1], F32)

        dma("sp", tri[:, :, :], tri_d[:, :, :], [], [tri], tri)
        dma("pool", identb[:, :], ident_d[:, :], [], [identb], identb)
        dma("sp", lamv[:, :, :], lamv_d[:, :, :], [], [lamv], lamv)
        dma("sp", dgs[:, :], dng_d[:, :], [], [dgs], dgs)
        dma("sp", gng[:, :], gng_d[:, :], [], [gng], gng)
        memset("dve", onesb, onesb[:, :], 1.0)
        memset("dve", onesf, onesf[:, :], 1.0)
        memset("dve", negsix, negsix[:, :], -1.0 / 16.0)
        ts("dve", dgs, dgs[:, :], dgs, dgs[:, :], 1.0 - LAM_INIT, None, ALU.mult)
        for i in range(2):
            tt("dve", lamt, lamt[:, :], lamv, lamv[:, 2 * i, :], lamv, lamv[:, 2 * i + 1, :], ALU.mult)
            S.op("dve", (lambda i: lambda e: e.reduce_sum(out=lams[:, i:i + 1], in_=lamt[:, :], axis=AX.X))(i),
                 reads=[lamt], writes=[lams])
        act(lams, lams[:, 2:4], lams, lams[:, 0:2], AF.Exp)
        tt("dve", neglam, neglam[:, :], lams, lams[:, 3:4], lams, lams[:, 2:3], ALU.subtract)
        ts("dve", neglam, neglam[:, :], neglam, neglam[:, :], -LAM_INIT, None, ALU.add)

        with ExitStack() as p1:
            Win = sbt(p1, "Win", [128, 8, D_IN], BF16)
            Wlr = sbt(p1, "Wlr", [128, 5, 8, 16], BF16)
            Wa2 = sbt(p1, "Wa2", [17, 5, 256], BF16)
            flags = sbt(p1, "flags", [128, 9], F32)
            w_in_v = w_in.rearrange("(c p) n -> p c n", p=128)
            for c in range(8):
                dma("pool", Win[:, c, :], w_in_v[:, c, :], [], [Win], Win, nowaw=True)
            for v in range(5):
                dma("pool", Wlr[:, v, :, :], wlr_d[v].rearrange("(c p) n -> p c n", p=128), [], [Wlr], Wlr, nowaw=True)
            dma("pool", Wa2[:, :, :], wa2_d[:, :, :], [], [Wa2], Wa2)
            dma("sp", flags[:, :], flags_d[:, :], [], [flags], flags)

            XTr = ring(p1, "XT", 2, [128, 8, 512], BF16)
            kstr = ring(p1, "kst", 2, [128, 512], BF16)
            vstr = ring(p1, "vst", 2, [128, 4, 512], BF16)
            LRaug = sbt(p1, "LRaug", [17, 512], BF16)
            t1r = ring(p1, "t1", 2, [128, 256], F32)
            lapr = ring(p1, "lap", 4, [128, 256], F32)
            e3 = sbt(p1, "e3", [128, 256], F32)
            khat = sbt(p1, "khat", [128, 256], BF16)
            vb = sbt(p1, "vb", [128, 512], BF16)
            dlast = sbt(p1, "dlast", [128, 2], F32)
            Sp = [sbt(p1, "Sp%d" % i, [128, 128], F32) for i in range(2)]
            Sbf = [sbt(p1, "Sbf%d" % i, [128, 128], BF16) for i in range(2)]
            SF = [sbt(p1, "SF%d" % i, [128, 128], F32) for i in range(2)]
            SB = [sbt(p1, "SB%d" % i, [128, 128], F32) for i in range(2)]
            gqT = sbt(p1, "gqT", [128, 2, 512], F32)
            gkT = sbt(p1, "gkT", [128, 2, 512], F32)
            E1 = [sbt(p1, "E1_%d" % i, [128, 128], F32) for i in range(2)]
            E2 = [sbt(p1, "E2_%d" % i, [128, 128], F32) for i in range(2)]
            qtl = [sbt(p1, "qtl%d" % i, [128, 128], BF16) for i in range(2)]
            ktl = [sbt(p1, "ktl%d" % i, [128, 128], BF16) for i in range(2)]
            amr = ring(p1, "am", 2, [128, 128], BF16)
            ofst = sbt(p1, "ofst", [128, 4, 512], F32)
            osum = sbt(p1, "osum", [128, 128], F32)
            sq = sbt(p1, "sq", [128, 128], F32)
            rstd = sbt(p1, "rstd", [128, 128], F32)
            srT = sbt(p1, "srT", [128, 4, 512], F32)
            srtmp = sbt(p1, "srtmp", [128, 512], F32)
            ogst = sbt(p1, "ogst", [128, 4, 512], BF16)

            pA = Ring([pst(p1, "pA%d" % i, [128, 512]) for i in range(2)])
            pG1 = pst(p1, "pG1", [128, 512])
            pG2 = pst(p1, "pG2", [128, 512])
            pE = pst(p1, "pE", [128, 512])
            pF = pst(p1, "pF", [128, 512])
            pO = pst(p1, "pO", [128, 512])
            pL = pst(p1, "pL", [128, 512])

            memset("dve", LRaug, LRaug[:, :], 1.0)

            def proj_fm(XT, col0, evac):
                pp = pA.next()
                for c in range(8):
                    mm(pp, pp[:, :], Win, Win[:, c, col0:col0 + 128], XT, XT[:, c, :], c == 0, c == 7)
                evac(pp)

            def gla_sub(XT, sub, variant, tri0, own, bwd, tok0, job, laps):
                cs = slice(sub * 128, (sub + 1) * 128)
                for c in range(8):
                    mm(pG1, pG1[:, :], XT, XT[:, c, cs], Win, Win[:, c, O_GK:O_GK + 512], c == 0, c == 7)
                for c in range(8):
                    mm(pG2, pG2[:, 0:256], XT, XT[:, c, cs], Win, Win[:, c, O_GK + 512:O_GK + 768], c == 0, c == 7)
                lap = laps[sub]
                if own:
                    for p in range(2):
                        mm(pF, pF[:, p * 128:(p + 1) * 128], lap, lap[:, p * 128:(p + 1) * 128], tri, tri[:, tri0, :], True, True)
                        act(E1[p], E1[p][:, :], pF, pF[:, p * 128:(p + 1) * 128], AF.Exp)
                        act(E2[p], E2[p][:, :], pF, pF[:, p * 128:(p + 1) * 128], AF.Exp, scale=-1.0)
                        stt("dve", qtl[p], qtl[p][:, :], gqT, gqT[:, p, cs], 0.125, E1[p], E1[p][:, :], ALU.mult, ALU.mult)
                        tt("dve", ktl[p], ktl[p][:, :], gkT, gkT[:, p, cs], E2[p], E2[p][:, :], ALU.mult)
                cp("act", vb, vb[:, 0:256], pG1, pG1[:, 256:512])
                cp("act", vb, vb[:, 256:512], pG2, pG2[:, 0:256])
                if own:
                    for h in range(4):
                        p, hh = h // 2, h % 2
                        rs = slice(hh * 64, (hh + 1) * 64)
                        if h % 2 == 0:
                            aTb, aT, oTb, oT, sS = pF, pF[:, 256:384], pO, pO[:, 0:128], pO[:, 128:256]
                        else:
                            aTb, aT, oTb, oT, sS = pL, pL[:, 0:128], pL, pL[:, 128:256], pL[:, 256:384]
                        mm(aTb, aT, ktl[p], ktl[p][rs, :], qtl[p], qtl[p][rs, :], True, True)
                        am = amr.next()
                        tt("dve", am, am[:, :], aTb, aT, tri, tri[:, tri0 + 2, :], ALU.mult)
                        mm(oTb, oT, Sbf[p], Sbf[p][rs, :], qtl[p], qtl[p][rs, :], True, False)
                        mm(oTb, oT, vb, vb[:, h * 128:(h + 1) * 128], am, am[:, :], False, True)
                        if not bwd:
                            cp("act", ofst, ofst[:, h, cs], oTb, oT)
                        else:
                            tt("dve", osum, osum[:, :], oTb, oT, ofst, ofst[:, h, cs], ALU.add)
                            tt("pool", sq, sq[:, :], osum, osum[:, :], osum, osum[:, :], ALU.mult)
                            mm(oTb, sS, onesf, onesf[:, :], sq, sq[:, :], True, True)
                            ts("dve", rstd, rstd[:, :], oTb, sS, 1.0 / 128.0, EPS, ALU.mult, ALU.add)
                            act(rstd, rstd[:, :], rstd, rstd[:, :], AF.Ln)
                            act(rstd, rstd[:, :], rstd, rstd[:, :], AF.Exp, scale=-0.5)
                            stt("dve", osum, osum[:, :], osum, osum[:, :], gng[:, 0:1], rstd, rstd[:, :], ALU.mult, ALU.mult,
                                extra_reads=[gng])
                            tt("dve", ogst, ogst[:, h, cs], osum, osum[:, :], srT, srT[:, h, cs], ALU.mult)
                mm(pE, pE[:, 0:256], tri, tri[:, tri0 + 1, :], lap, lap[:, :], True, True)
                act(e3, e3[:, :], pE, pE[:, 0:256], AF.Exp)
                tt("dve", khat, khat[:, :], pG1, pG1[:, 0:256], e3, e3[:, :], ALU.mult)
                for p in range(2):
                    mm(pF, pF[:, 384 + p:385 + p], lap, lap[:, p * 128:(p + 1) * 128], negsix, negsix[:, 0:1], True, True)
                act(dlast, dlast[:, 0:2], pF, pF[:, 384:386], AF.Exp)
                for p in range(2):
                    mm(pE, pE[:, 256:512], khat, khat[:, p * 128:(p + 1) * 128], vb, vb[:, p * 256:(p + 1) * 256], True, True)
                    for hh in range(2):
                        rs = slice(hh * 64, (hh + 1) * 64)
                        stt("dve", Sp[p], Sp[p][rs, :], Sp[p], Sp[p][rs, :], dlast[rs, p:p + 1],
                            pE, pE[rs, 256 + hh * 128:256 + (hh + 1) * 128], ALU.mult, ALU.add, extra_reads=[dlast])
                    if own:
                        cp("pool", Sbf[p], Sbf[p][:, :], Sp[p], Sp[p][:, :])

            def gate_z(variant):
                laps = []
                for sub in range(4):
                    cs = slice(sub * 128, (sub + 1) * 128)
                    zc = slice((sub % 2) * 256, (sub % 2) * 256 + 256)
                    mm(pL, pL[:, zc], LRaug, LRaug[0:17, cs], Wa2, Wa2[0:17, variant, :], True, True)
                    lap, t1 = lapr.next(), t1r.next()
                    act(t1, t1[:, :], pL, pL[:, zc], AF.Exp, scale=-1.0)
                    act(lap, lap[:, :], t1, t1[:, :], AF.Ln, bias=1.0)
                    laps.append(lap)
                return laps

            def lr_proj(XT, variant):
                for c in range(8):
                    mm(pL, pL[0:16, :], Wlr, Wlr[:, variant, c, :], XT, XT[:, c, :], c == 0, c == 7)
                cp("act", LRaug, LRaug[0:16, :], pL, pL[0:16, :])

            def kv_proj(XT, job, key0, part):
                for h in (range(4) if part == "k" else []):
                    def evac(pp, h=h):
                        kst = kstr.next()
                        cp("act", kst, kst[:, :], pp, pp[:, :])
                        dma("sp", Kscr[job][h].rearrange("c r k -> (c r) k")[:, key0:key0 + 512], kst[:, :], [kst], [], kst)
                    proj_fm(XT, O_DK + h * 128, evac)
                if part == "k":
                    return
                vst = vstr.next()
                for sub in range(4):
                    pp = pA.next()
                    for c in range(8):
                        mm(pp, pp[:, :], XT, XT[:, c, sub * 128:(sub + 1) * 128], Win, Win[:, c, O_DV:O_DV + 512], c == 0, c == 7)
                    cp("dve", vst, vst[:, sub, :], pp, pp[:, :])
                kt0 = key0 // 128
                dma("sp", Vscr[job][kt0:kt0 + 4].rearrange("k p n -> p k n"), vst[:, :, :], [vst], [], vst)

            def load_xT(src_ap, t0):
                XT = XTr.next()
                dma("pool", XT[:, :, :], src_ap.rearrange("(c p) t -> p c t", p=128)[:, :, t0:t0 + 512], [], [XT], XT)
                return XT

            for job in ("p", "s"):
                for p in range(2):
                    memset("dve", Sp[p], Sp[p][:, :], 0.0)
                    memset("dve", SF[p], SF[p][:, :], 0.0)
                    memset("dve", SB[p], SB[p][:, :], 0.0)
                if job == "p":
                    for slot in range(3):
                        for ti in range(NT):
                            XT = load_xT(xoT[slot], ti * 512)
                            lr_proj(XT, slot)
                            kv_proj(XT, job, slot * L + ti * 512, "k")
                            laps = gate_z(slot)
                            kv_proj(XT, job, slot * L + ti * 512, "v")
                            for sub in range(4):
                                gla_sub(XT, sub, slot, 0, False, False, 0, job, laps)
                        for p in range(2):
                            stt("dve", SF[p], SF[p][:, :], Sp[p], Sp[p][:, :], flags[:, slot:slot + 1], SF[p], SF[p][:, :],
                                ALU.mult, ALU.add, extra_reads=[flags])
                            stt("dve", SB[p], SB[p][:, :], Sp[p], Sp[p][:, :], flags[:, 3 + slot:4 + slot], SB[p], SB[p][:, :],
                                ALU.mult, ALU.add, extra_reads=[flags])
                            ts("dve", Sp[p], Sp[p][:, :], Sp[p], Sp[p][:, :], flags[:, 6 + slot:7 + slot], None, ALU.mult,
                               extra_reads=[flags])
                for p in range(2):
                    cp("dve", Sp[p], Sp[p][:, :], SF[p], SF[p][:, :])
                    cp("pool", Sbf[p], Sbf[p][:, :], SF[p], SF[p][:, :])
                for ti in range(NT):
                    XT = load_xT(xT[job], ti * 512)
                    lr_proj(XT, 3)
                    kv_proj(XT, job, OWN0[job] + ti * 512, "k")
                    laps = gate_z(3)
                    kv_proj(XT, job, OWN0[job] + ti * 512, "v")
                    for h in range(4):
                        def evacq(pp, h=h, ti=ti):
                            kst = kstr.next()
                            S.op("act", (lambda o, i: lambda e: e.mul(out=o, in_=i, mul=0.125))(kst[:, :], pp[:, :]),
                                 reads=[pp], writes=[kst])
                            dma("sp", Qscr[job][h].rearrange("c r k -> (c r) k")[:, ti * 512:(ti + 1) * 512], kst[:, :], [kst], [], kst)
                        proj_fm(XT, O_DQ + h * 128, evacq)
                    for p in range(2):
                        proj_fm(XT, O_GQ + p * 128, lambda pp, p=p: cp("act", gqT, gqT[:, p, :], pp, pp[:, :]))
                        proj_fm(XT, O_GK + p * 128, lambda pp, p=p: cp("act", gkT, gkT[:, p, :], pp, pp[:, :]))
                    for sub in range(4):
                        gla_sub(XT, sub, 3, 0, True, False, ti * 512, job, laps)
                    dma("sp", OFscr[job].rearrange("h p t -> p h t")[:, :, ti * 512:(ti + 1) * 512], ofst[:, :, :], [ofst], [], ofst)
                for p in range(2):
                    cp("dve", Sp[p], Sp[p][:, :], SB[p], SB[p][:, :])
                    cp("pool", Sbf[p], Sbf[p][:, :], SB[p], SB[p][:, :])
                for ti in reversed(range(NT)):
                    XT = load_xT(xT[job], ti * 512)
                    dma("sp", ofst[:, :, :], OFscr[job].rearrange("h p t -> p h t")[:, :, ti * 512:(ti + 1) * 512], [], [ofst], ofst)
                    lr_proj(XT, 4)
                    for p in range(2):
                        proj_fm(XT, O_GQ + p * 128, lambda pp, p=p: cp("act", gqT, gqT[:, p, :], pp, pp[:, :]))
                        proj_fm(XT, O_GK + p * 128, lambda pp, p=p: cp("act", gkT, gkT[:, p, :], pp, pp[:, :]))
                    laps = gate_z(4)
                    for h in range(4):
                        def evacr(pp, h=h):
                            act(srtmp, srtmp[:, :], pp, pp[:, :], AF.Exp, scale=-1.0)
                            ts("pool", srtmp, srtmp[:, :], srtmp, srtmp[:, :], 1.0, None, ALU.add)
                            S.op("dve", lambda e: e.reciprocal(out=srtmp[:, :], in_=srtmp[:, :]), reads=[srtmp], writes=[srtmp])
                            tt("dve", srT, srT[:, h, :], pp, pp[:, :], srtmp, srtmp[:, :], ALU.mult)
                        proj_fm(XT, O_GR + h * 128, evacr)
                    for sub in reversed(range(4)):
                        gla_sub(XT, sub, 4, 3, True, True, ti * 512, job, laps)
                    dma("sp", CATscr[job][512:1024, :].rearrange("(h p) t -> p h t", p=128)[:, :, ti * 512:(ti + 1) * 512],
                        ogst[:, :, :], [ogst], [], ogst)
            S.barrier()

        with ExitStack() as p2:
            NKmax = NK["p"]
            Kb = [sbt(p2, "Kb%d" % c, [66, NKmax], BF16) for c in range(2)]
            Vh = sbt(p2, "Vh", [128, NKmax // 128, 128], BF16)
            dtab = sbt(p2, "dtab", [128, 4, 512], F32)
            biasBA = sbt(p2, "biasBA", [128, 4, 2, NKO + 1], F32)
            biasO = sbt(p2, "biasO", [128, 4, NT, 3 * NKO], F32)
            PTr = ring(p2, "PT", 8, [128, 2, 512], BF16)
            tq1r = ring(p2, "tq1", 2, [128, 2, 512], F32)
            tq2r = ring(p2, "tq2", 2, [128, 2, 512], F32)
            sbr = ring(p2, "sbq", 3, [128, 2, 512], BF16)
            smr = ring(p2, "sm", 2, [128, 2, 512], F32)
            r0 = sbt(p2, "r0", [128, 512], F32)
            r1 = sbt(p2, "r1", [128, 512], F32)
            u0 = sbt(p2, "u0", [128, 512], F32)
            u1 = sbt(p2, "u1", [128, 512], F32)
            o0s = sbt(p2, "o0s", [128, 512], F32)
            o1s = sbt(p2, "o1s", [128, 512], F32)
            odst = ring(p2, "odst", 2, [128, 512], BF16)
            pS = Ring([pst(p2, "pS%d" % i, [128, 2, 512]) for i in range(2)])
            pOc = [pst(p2, "pOc%d" % c, [128, 512]) for c in range(2)]
            pDc = [pst(p2, "pDc%d" % c, [128, 512]) for c in range(2)]
            Qsets = [([sbt(p2, "QB%d_%d" % (c, i), [66, 512], BF16) for c in range(2)],
                      [sbt(p2, "QA%d_%d" % (c, i), [66, 512], BF16) for c in range(2)]) for i in range(2)]

            dma("sp", dtab[:, :, :], dtab_d[:, :, :], [], [dtab], dtab)
            dma("sp", biasBA[:, :, :, :], biasBA_d[:, :, :, :], [], [biasBA], biasBA)
            dma("sp", biasO[:, :, :, :], biasO_d[:, :, :, :], [], [biasO], biasO)

            def min_dist(job, qi, kt):
                own_kt0_ = OWN0[job] // 128
                t0 = qi * 512
                if kt >= own_kt0_:
                    s0 = (kt - own_kt0_) * 128
                    if s0 + 128 <= t0:
                        return t0 - (s0 + 127)
                    if s0 >= t0 + 512:
                        return s0 - (t0 + 511)
                    return 0
                slot, ktl = kt // NKO, kt % NKO
                best = None
                for j in range(4):
                    order = [(i, True) for i in range(j)] + [(i, False) for i in range(3, j, -1)]
                    i, isb = order[slot]
                    if isb:
                        d = (j * L + t0) - (i * L + 128 * ktl + 127)
                    else:
                        d = (i * L + L - 128 * ktl - 128) - (j * L + t0 + 511)
                    best = d if best is None else min(best, d)
                return best

            def active_tiles(job, h, qi):
                nkt_ = NK[job] // 128
                lim = (104.0 + 64.0) / SLOPES[h]
                dist = [min_dist(job, qi, kt) for kt in range(nkt_)]
                keep = [kt for kt in range(nkt_) if dist[kt] <= lim]
                extra = sorted((kt for kt in range(nkt_) if dist[kt] > lim), key=lambda kt: dist[kt])
                while len(keep) % 4 != 0 or len(keep) < 4:
                    keep.append(extra.pop(0))
                return sorted(keep)

            def load_q(job, h, qi):
                QBs, QAs = Qsets[qi % 2]
                for c in range(2):
                    dma("sp", QBs[c][0:64, :], Qscr[job][h, c, :, qi * 512:(qi + 1) * 512], [], [QBs[c]], QBs[c])
                    dma("sp", QAs[c][0:64, :], Qscr[job][h, c, :, qi * 512:(qi + 1) * 512], [], [QAs[c]], QAs[c])

            for job in ("p", "s"):
                nk = NK[job]
                nkt = nk // 128
                own_kt0 = OWN0[job] // 128
                for c in range(2):
                    dma("sp", Kb[c][64:66, 0:nk], kaug_d[job][:, :], [], [Kb[c]], Kb[c])
                for h in range(4):
                    for c in range(2):
                        dma("sp", Kb[c][0:64, 0:nk], Kscr[job][h, c], [], [Kb[c]], Kb[c])
                        for QBs, QAs in Qsets:
                            dma("sp", QBs[c][64:66, :], qaug_d[:, h, 0, :], [], [QBs[c]], QBs[c])
                            dma("sp", QAs[c][64:66, :], qaug_d[:, h, 1, :], [], [QAs[c]], QAs[c])
                    for k0 in range(0, nkt, 16):
                        k1 = min(nkt, k0 + 16)
                        dma("sp", Vh[:, k0:k1, :], Vscr[job][k0:k1, :, h * 128:(h + 1) * 128].rearrange("k p e -> p k e"),
                            [], [Vh], Vh, nowaw=(k0 > 0))
                    load_q(job, h, 0)
                    def make_tile(qi, job=job, h=h):
                        QB, QA = Qsets[qi % 2]
                        st = {}

                        act_kts = active_tiles(job, h, qi)
                        n_act = len(act_kts)

                        def qk(i, qi=qi, QB=QB, QA=QA, st=st, h=h, job=job, act_kts=act_kts):
                            kt = act_kts[i]
                            if kt >= own_kt0:
                                s0 = (kt - own_kt0) * 128
                                t0 = qi * 512
                                if s0 + 128 <= t0:
                                    cls, Qs, R = "B", QB, 66
                                    jx = (t0 - s0) // 128
                                    bias_ap, bias_b = biasBA[:, h, 0, jx:jx + 1], biasBA
                                elif s0 >= t0 + 512:
                                    cls, Qs, R = "A", QA, 66
                                    jx = (s0 - t0) // 128
                                    bias_ap, bias_b = biasBA[:, h, 1, jx:jx + 1], biasBA
                                else:
                                    cls, Qs, R = "D", QB, 64
                                    bias_ap, bias_b = (s0 - t0) // 128, None
                            else:
                                cls, Qs, R = "O", QB, 66
                                bias_ap, bias_b = biasO[:, h, qi, kt:kt + 1], biasO
                            ps_ = pS.next()
                            for c in range(2):
                                mm(ps_, ps_[:, c, :], Kb[c], Kb[c][0:R, kt * 128:(kt + 1) * 128], Qs[c], Qs[c][0:R, :], True, True)
                            PT = PTr.next()
                            if cls == "D":
                                sm = smr.next()
                                for c in range(2):
                                    stt("dve", sm, sm[:, c, :], dtab, dtab[:, bias_ap, :], -SLOPES[h], ps_, ps_[:, c, :],
                                        ALU.mult, ALU.add)
                                act(PT, PT[:, :, :], sm, sm[:, :, :], AF.Exp)
                            else:
                                act(PT, PT[:, :, :], ps_, ps_[:, :, :], AF.Exp, bias=bias_ap, extra_reads=[bias_b])
                            st[i] = PT

                        pend = []

                        def pv(i, st=st, pend=pend, act_kts=act_kts, n_act=n_act):
                            kt_v = act_kts[i]
                            kt = i
                            PT = st[kt]
                            first, last = kt == 0, kt == n_act - 1
                            for c in range(2):
                                mm(pOc[c], pOc[c][:, :], Vh, Vh[:, kt_v, :], PT, PT[:, c, :], first, last)
                            if kt % 4 == 1:
                                a, b = st.pop(kt - 1), st.pop(kt)
                                t1 = tq1r.next()
                                tt("dve", t1, t1[:, :, :], a, a[:, :, :], b, b[:, :, :], ALU.add)
                                st["t1"] = t1
                            if kt % 4 == 3:
                                c_, d = st.pop(kt - 1), st.pop(kt)
                                t1, t2, sb = st.pop("t1"), tq2r.next(), sbr.next()
                                tt("dve", t2, t2[:, :, :], c_, c_[:, :, :], d, d[:, :, :], ALU.add)
                                tt("dve", sb, sb[:, :, :], t1, t1[:, :, :], t2, t2[:, :, :], ALU.add)
                                pend.append((kt + 2, sb, kt == 3, last))
                            while pend and (pend[0][0] <= kt or last):
                                _, sb_, f_, l_ = pend.pop(0)
                                for c in range(2):
                                    mm(pDc[c], pDc[c][:, :], onesb, onesb[:, :], sb_, sb_[:, c, :], f_, l_)

                        LA = 3

                        def prologue():
                            for i in range(min(LA, n_act)):
                                qk(i)

                        def body(hook=None, hook_at=6):
                            if qi + 1 < NT:
                                load_q(job, h, qi + 1)
                            for i in range(n_act):
                                if i + LA < n_act:
                                    qk(i + LA)
                                pv(i)
                                if hook is not None and i == min(hook_at, n_act - 1):
                                    hook()

                        def epi_a():
                            cp("dve", o0s, o0s[:, :], pOc[0], pOc[0][:, :])
                            cp("dve", o1s, o1s[:, :], pOc[1], pOc[1][:, :])
                            S.op("dve", lambda e: e.reciprocal(out=r0[:, :], in_=pDc[0][:, :]), reads=[pDc[0]], writes=[r0])
                            S.op("dve", lambda e: e.reciprocal(out=r1[:, :], in_=pDc[1][:, :]), reads=[pDc[1]], writes=[r1])
                            tt("dve", u0, u0[:, :], o0s, o0s[:, :], r0, r0[:, :], ALU.mult)
                            tt("dve", u1, u1[:, :], o1s, o1s[:, :], r1, r1[:, :], ALU.mult)
                            stt("dve", u0, u0[:, :], u1, u1[:, :], neglam[:, 0:1], u0, u0[:, :], ALU.mult, ALU.add, extra_reads=[neglam])
                            tt("pool", u1, u1[:, :], u0, u0[:, :], u0, u0[:, :], ALU.mult)

                        def epi_b():
                            pq = pS.next()
                            mm(pq, pq[:, 0, :], onesf, onesf[:, :], u1, u1[:, :], True, True)
                            ts("dve", r0, r0[:, :], pq, pq[:, 0, :], 1.0 / 128.0, EPS, ALU.mult, ALU.add)
                            act(r0, r0[:, :], r0, r0[:, :], AF.Ln)
                            act(r0, r0[:, :], r0, r0[:, :], AF.Exp, scale=-0.5)
                            od = odst.next()
                            stt("dve", od, od[:, :], u0, u0[:, :], dgs[:, 0:1], r0, r0[:, :], ALU.mult, ALU.mult, extra_reads=[dgs])
                            dma("sp", CATscr[job][h * 128:(h + 1) * 128, qi * 512:(qi + 1) * 512], od[:, :], [od], [], od)

                        return prologue, body, epi_a, epi_b

                    tls = [make_tile(qi) for qi in range(NT)]
                    tls[0][0]()
                    hook = None
                    for qi in range(NT):
                        tls[qi][1](hook)
                        tls[qi][2]()
                        if qi + 1 < NT:
                            tls[qi + 1][0]()
                            hook = tls[qi][3]
                        else:
                            tls[qi][3]()
                            hook = None
            S.barrier()

        def ln_a(es_bufs, y1):
            stats, mv, rs_ = es_bufs
            for hf in range(2):
                S.op("dve", (lambda hf: lambda e: e.bn_stats(out=stats[:, hf * 6:(hf + 1) * 6], in_=y1[:, hf * 512:(hf + 1) * 512]))(hf),
                     reads=[y1], writes=[stats])
            S.op("dve", lambda e: e.bn_aggr(out=mv[:, :], in_=stats[:, :]), reads=[stats], writes=[mv])
            ts("dve", rs_, rs_[:, :], mv, mv[:, 1:2], EPS, None, ALU.add)
            act(rs_, rs_[:, :], rs_, rs_[:, :], AF.Ln)
            act(rs_, rs_[:, :], rs_, rs_[:, :], AF.Exp, scale=-0.5)

        def ln_b(es_bufs, y1, lngb, outb, out_ap):
            stats, mv, rs_ = es_bufs
            ts("dve", y1, y1[:, :], y1, y1[:, :], mv[:, 0:1], rs_[:, 0:1], ALU.subtract, ALU.mult, extra_reads=[mv, rs_])
            tt("dve", y1, y1[:, :], y1, y1[:, :], lngb, lngb[:, 0, :], ALU.mult)
            tt("dve", outb, out_ap, y1, y1[:, :], lngb, lngb[:, 1, :], ALU.add)

        with ExitStack() as p3:
            lnbr = Ring([(sbt(p3, "stats%d" % i, [128, 12], F32), sbt(p3, "mv%d" % i, [128, 2], F32),
                          sbt(p3, "rs%d" % i, [128, 1], F32)) for i in range(3)])
            with ExitStack() as p3a:
                Wo = sbt(p3a, "Wo", [128, 8, D_MODEL], BF16)
                lngb1 = sbt(p3a, "lngb1", [128, 2, D_MODEL], F32)
                dma("sp", lngb1[:, :, :], lngb_d[:, 0:2, :], [], [lngb1], lngb1)
                w_o_v = w_o.rearrange("(c p) n -> p c n", p=128)
                for c in range(8):
                    dma("pool", Wo[:, c, :], w_o_v[:, c, :], [], [Wo], Wo, nowaw=True)
                CTr = ring(p3a, "CT", 2, [128, 8, 512], BF16)
                xrr = ring(p3a, "xres", 5, [128, D_MODEL], F32)
                y1r = ring(p3a, "y1", 3, [128, D_MODEL], F32)
                x1r = ring(p3a, "x1", 3, [128, D_MODEL], F32)
                x1b = sbt(p3a, "x1b", [128, D_MODEL], BF16)
                x1Tr = ring(p3a, "x1T", 2, [128, 8, 128], BF16)
                pMr = Ring([[pst(p3a, "pM%d_%d" % (i, k), [128, 512]) for i in range(2)] for k in range(2)])
                pT = pst(p3a, "pT", [128, 8, 128], BF16)
                subs3a = [(job, ti, sub) for job in ("p", "s") for ti in range(NT) for sub in range(4)]
                st3a = {}

                xres_of = {}

                def load_xres(k):
                    job, ti, sub = subs3a[k]
                    tok0 = ti * 512 + sub * 128
                    xres = xrr.next()
                    dma("sp", xres[:, :], xr[job][tok0:tok0 + 128, :], [], [xres], xres)
                    xres_of[k] = xres

                def a3(k):
                    job, ti, sub = subs3a[k]
                    if sub == 0:
                        CT = CTr.next()
                        dma("sp", CT[:, :, :], CATscr[job].rearrange("(c p) t -> p c t", p=128)[:, :, ti * 512:(ti + 1) * 512],
                            [], [CT], CT)
                        st3a["CT"] = CT
                    CT = st3a["CT"]
                    tok0 = ti * 512 + sub * 128
                    if k + 2 < len(subs3a):
                        load_xres(k + 2)
                    xres = xres_of.pop(k)
                    y1 = y1r.next()
                    pM = pMr.next()
                    for hf in range(2):
                        for c in range(8):
                            mm(pM[hf], pM[hf][:, :], CT, CT[:, c, sub * 128:(sub + 1) * 128],
                               Wo, Wo[:, c, hf * 512:(hf + 1) * 512], c == 0, c == 7)
                        stt("dve", y1, y1[:, hf * 512:(hf + 1) * 512], xres, xres[:, hf * 512:(hf + 1) * 512], ALPHA,
                            pM[hf], pM[hf][:, :], ALU.mult, ALU.add)
                    lnb = lnbr.next()
                    ln_a(lnb, y1)
                    st3a[k] = (y1, lnb)

                st3b = {}

                def b3a(k):
                    job, ti, sub = subs3a[k]
                    tok0 = ti * 512 + sub * 128
                    y1, lnb = st3a.pop(k)
                    x1 = x1r.next()
                    ln_b(lnb, y1, lngb1, x1, x1[:, :])
                    dma("sp", X1scr[job][tok0:tok0 + 128, :], x1[:, :], [x1], [], x1)
                    st3b[k] = x1

                def b3b(k):
                    job, ti, sub = subs3a[k]
                    tok0 = ti * 512 + sub * 128
                    x1 = st3b.pop(k)
                    cp("act", x1b, x1b[:, :], x1, x1[:, :])
                    for c in range(8):
                        S.op("pe", (lambda c: lambda e: e.transpose(out=pT[:, c, :], in_=x1b[:, c * 128:(c + 1) * 128],
                                                                    identity=identb[:, :]))(c),
                             reads=[x1b, identb], writes=[pT])
                    x1T = x1Tr.next()
                    cp("act", x1T, x1T[:, :, :], pT, pT[:, :, :])
                    dma("sp", X1Tscr[job].rearrange("(c p) t -> p c t", p=128)[:, :, tok0:tok0 + 128], x1T[:, :, :],
                        [x1T], [], x1T)

                n3 = len(subs3a)
                load_xres(0)
                if n3 > 1:
                    load_xres(1)
                a3(0)
                if n3 > 1:
                    a3(1)
                b3a(0)
                for k in range(n3):
                    if k + 2 < n3:
                        a3(k + 2)
                    if k + 1 < n3:
                        b3a(k + 1)
                    b3b(k)
                S.barrier()

            with ExitStack() as p3b:
                W1 = sbt(p3b, "W1", [128, 8, D_FF], BF16)
                lngb2 = sbt(p3b, "lngb2", [128, 2, D_MODEL], F32)
                dma("sp", lngb2[:, :, :], lngb_d[:, 2:4, :], [], [lngb2], lngb2)
                W2 = sbt(p3b, "W2", [128, 32, D_MODEL], BF16)
                w1v = w_ff1.rearrange("(c p) n -> p c n", p=128)
                w2v = w_ff2.rearrange("(c p) n -> p c n", p=128)
                for c in range(8):
                    dma("pool", W1[:, c, :], w1v[:, c, :], [], [W1], W1, nowaw=True)
                for c in range(32):
                    dma("pool", W2[:, c, :], w2v[:, c, :], [], [W2], W2, nowaw=True)
                X1Tr = ring(p3b, "X1T", 1, [128, 8, 512], BF16)
                HT = sbt(p3b, "HT", [128, 32, 512], BF16)
                rlr = ring(p3b, "rl", 2, [128, 512], F32)
                x1rr = ring(p3b, "x1res", 1, [128, D_MODEL], F32)
                y2r = ring(p3b, "y2", 2, [128, D_MODEL], F32)
                yor = ring(p3b, "yo", 2, [128, D_MODEL], F32)
                pH = Ring([pst(p3b, "pH%d" % i, [128, 512]) for i in range(3)])
                pO2 = [pst(p3b, "pO2%d" % i, [128, 512]) for i in range(2)]
                pend3b = []
                for job in ("p", "s"):
                    for ti in range(NT):
                        X1T = X1Tr.next()
                        dma("sp", X1T[:, :, :], X1Tscr[job].rearrange("(c p) t -> p c t", p=128)[:, :, ti * 512:(ti + 1) * 512],
                            [], [X1T], X1T)
                        for f in range(32):
                            ph = pH.next()
                            for c in range(8):
                                mm(ph, ph[:, :], W1, W1[:, c, f * 128:(f + 1) * 128], X1T, X1T[:, c, :], c == 0, c == 7)
                            rl = rlr.next()
                            act(rl, rl[:, :], ph, ph[:, :], AF.Relu)
                            tt("pool", HT, HT[:, f, :], rl, rl[:, :], rl, rl[:, :], ALU.mult)
                        for sub in range(4):
                            tok0 = ti * 512 + sub * 128
                            x1res = x1rr.next()
                            dma("sp", x1res[:, :], X1scr[job][tok0:tok0 + 128, :], [], [x1res], x1res)
                            y2 = y2r.next()
                            for hf in range(2):
                                for f in range(32):
                                    mm(pO2[hf], pO2[hf][:, :], HT, HT[:, f, sub * 128:(sub + 1) * 128],
                                       W2, W2[:, f, hf * 512:(hf + 1) * 512], f == 0, f == 31)
                                stt("dve", y2, y2[:, hf * 512:(hf + 1) * 512], x1res, x1res[:, hf * 512:(hf + 1) * 512], ALPHA,
                                    pO2[hf], pO2[hf][:, :], ALU.mult, ALU.add)
                            lnb = lnbr.next()
                            ln_a(lnb, y2)
                            if pend3b:
                                pend3b.pop(0)()

                            def fin(lnb=lnb, y2=y2, job=job, tok0=tok0):
                                yo = yor.next()
                                ln_b(lnb, y2, lngb2, yo, yo[:, :])
                                dma("sp", yout[job][tok0:tok0 + 128, :], yo[:, :], [yo], [], yo)
                            pend3b.append(fin)
                while pend3b:
                    pend3b.pop(0)()
                S.barrier()

        S.emit(newsem)
    return nc


def _consts(L):
    NT, NKO = L // 512, L // 128
    k = np.arange(128)
    jj, ii = np.meshgrid(k, k, indexing="ij")
    tri = np.zeros((128, 6, 128), np.float32)
    tri[:, 0] = np.where(jj <= ii, -1.0 / 16, 0.0)
    tri[:, 1] = np.where(jj > ii, -1.0 / 16, 0.0)
    tri[:, 2] = np.where(jj <= ii, 1.0, 0.0)
    tri[:, 3] = np.where(jj >= ii, -1.0 / 16, 0.0)
    tri[:, 4] = np.where(jj < ii, -1.0 / 16, 0.0)
    tri[:, 5] = np.where(jj >= ii, 1.0, 0.0)
    ident = np.eye(128, dtype=np.float32)
    q = np.arange(512)
    dtab = np.zeros((128, 4, 512), np.float32)
    for j in range(4):
        dtab[:, j, :] = np.abs(q[None, :] - (128 * j + k[:, None]))
    biasBA = np.zeros((128, 4, 2, NKO + 1), np.float32)
    jv = np.arange(NKO + 1)
    for h in range(4):
        biasBA[:, h, 0, :] = -SLOPES[h] * (128.0 * jv[None, :] - k[:, None])
        biasBA[:, h, 1, :] = -SLOPES[h] * (128.0 * jv[None, :] + k[:, None])
    t_hi = (q // 2) * 2
    t_lo = q % 2
    qaug = np.zeros((2, 4, 2, 512), np.float32)
    for h in range(4):
        qaug[0, h, 0] = -SLOPES[h] * t_hi
        qaug[1, h, 0] = -SLOPES[h] * t_lo
        qaug[0, h, 1] = SLOPES[h] * t_hi
        qaug[1, h, 1] = SLOPES[h] * t_lo
    return tri, ident, dtab, biasBA, qaug


def make_in_maps(L, x_prompt, x_sample, w_in, w_o, lam_q1, lam_k1, lam_q2, lam_k2, diff_norm_g,
                 gla_wa2_f, gla_ba_f, gla_wa2_b, gla_ba_b, gla_norm_g, ln1_g, ln1_b, w_ff1, w_ff2, ln2_g, ln2_b):
    NT, NKO = L // 512, L // 128
    f = np.float32
    w_in0 = np.ascontiguousarray(w_in[0], dtype=f)
    tri, ident, dtab, biasBA, qaug = _consts(L)
    lamv = np.ascontiguousarray(np.broadcast_to(
        np.stack([lam_q1[0], lam_k1[0], lam_q2[0], lam_k2[0]])[None], (128, 4, 64)), dtype=f)
    lngb = np.ascontiguousarray(np.broadcast_to(
        np.stack([ln1_g[0], ln1_b[0], ln2_g[0], ln2_b[0]])[None], (128, 4, D_MODEL)), dtype=f)
    wa2aug_f = np.concatenate([gla_wa2_f[0], gla_ba_f[0][None]], 0)
    wa2aug_b = np.concatenate([gla_wa2_b[0], gla_ba_b[0][None]], 0)
    wlr_f = w_in0[:, O_LRF:O_LRB]
    wlr_b = w_in0[:, O_LRB:D_IN]
    shared = {
        "w_in": w_in0, "w_o": np.ascontiguousarray(w_o[0], dtype=f),
        "w_ff1": np.ascontiguousarray(w_ff1[0], dtype=f), "w_ff2": np.ascontiguousarray(w_ff2[0], dtype=f),
        "lamv": lamv, "dng": np.ascontiguousarray(diff_norm_g[0].reshape(128, 1), dtype=f),
        "gng": np.ascontiguousarray(gla_norm_g[0].reshape(128, 1), dtype=f), "lngb": lngb,
        "tri": tri, "ident": ident, "dtab": dtab, "biasBA": biasBA, "qaug": qaug.astype(ml_dtypes.bfloat16),
        "kaugs": np.ones((2, L), ml_dtypes.bfloat16),
    }
    k = np.arange(128)
    in_maps = []
    for c in range(8):
        b, j = c // 4, c % 4
        before = list(range(j))
        after = list(range(3, j, -1))
        slots = [(i, True) for i in before] + [(i, False) for i in after]
        nb = len(before)
        xo = np.zeros((3, D_MODEL, L), f)
        kaugp = np.ones((2, 4 * L), f)
        biasO = np.zeros((128, 4, NT, 3 * NKO), f)
        wlr = np.zeros((5, D_MODEL, 16), f)
        wa2 = np.zeros((17, 5, 256), f)
        for s, (i, isb) in enumerate(slots):
            seg = x_prompt[b, i * L:(i + 1) * L]
            pos = np.arange(i * L, (i + 1) * L)
            if not isb:
                seg = seg[::-1]
                pos = pos[::-1]
                kaugp[:, s * L:(s + 1) * L] = -1.0
            xo[s] = seg.T
            wlr[s] = wlr_f if isb else wlr_b
            wa2[:, s] = wa2aug_f if isb else wa2aug_b
            posm = pos.reshape(NKO, 128)
            for qi in range(NT):
                t0 = j * L + qi * 512
                d = (t0 - posm) if isb else (posm - t0)
                for h in range(4):
                    biasO[:, h, qi, s * NKO:(s + 1) * NKO] = (-SLOPES[h] * d.astype(f)).T
        wlr[3], wlr[4] = wlr_f, wlr_b
        wa2[:, 3], wa2[:, 4] = wa2aug_f, wa2aug_b
        flags = np.zeros((128, 9), f)
        for s in range(3):
            flags[:, s] = 1.0 if s == nb - 1 else 0.0
            flags[:, 3 + s] = 1.0 if (s == 2 and nb < 3) else 0.0
            flags[:, 6 + s] = 0.0 if s == nb - 1 else 1.0
        own = x_prompt[b, j * L:(j + 1) * L]
        m = dict(shared)
        m.update({
            "xpT": np.ascontiguousarray(own.T, dtype=f), "xp": np.ascontiguousarray(own, dtype=f),
            "xsT": np.ascontiguousarray(x_sample[c].T, dtype=f), "xs": np.ascontiguousarray(x_sample[c], dtype=f),
            "xoT": xo, "wlr": wlr, "wa2aug": wa2, "biasO": biasO, "kaugp": kaugp.astype(ml_dtypes.bfloat16), "flags": flags,
        })
        in_maps.append(m)
    return in_maps


_NC_CACHE = {}


def run(L, **inputs):
    inputs = {k_: np.asarray(v) for k_, v in inputs.items()}
    if L not in _NC_CACHE:
        _NC_CACHE[L] = build(L)
    nc = _NC_CACHE[L]
    in_maps = make_in_maps(L, **inputs)
    res = run_bass_kernel_spmd(nc, in_maps, core_ids=list(range(8)))
    yp = np.zeros((2, 4 * L, D_MODEL), np.float32)
    ys = np.zeros((8, L, D_MODEL), np.float32)
    for c in range(8):
        b, j = c // 4, c % 4
        yp[b, j * L:(j + 1) * L] = res.results[c]["yp"]
        ys[c] = res.results[c]["ys"]
    return yp, ys


def kernel(**inputs):
    return run(FULL_L, **inputs)
```

```python
import math
from contextlib import ExitStack

import numpy as np
import ml_dtypes
import concourse.bass as bass
import concourse.mybir as mybir
from concourse.bass_utils import run_bass_kernel_spmd

F32 = mybir.dt.float32
BF16 = mybir.dt.bfloat16
AF = mybir.ActivationFunctionType
ALU = mybir.AluOpType
AX = mybir.AxisListType

D_MODEL = 1024
D_IN = 3104
D_FF = 4096
O_DQ, O_DK, O_DV, O_GQ, O_GK, O_GV, O_GR, O_LRF, O_LRB = 0, 512, 1024, 1536, 1792, 2048, 2560, 3072, 3088
EPS = 1e-5
ALPHA = 2.0 ** 0.25
LAM_INIT = 0.8 - 0.6 * math.exp(0.0)
SLOPES = [2.0 ** (-8.0 * (h + 1) / 4) for h in range(4)]
FULL_L = 4096

SEM_CAP = 16000
ENGS = ("pe", "act", "dve", "pool", "sp")


class Buf:
    __slots__ = ("name", "t", "last_w", "readers", "dma_sem", "dma_cnt")

    def __init__(self, name, t=None):
        self.name = name
        self.t = t
        self.last_w = None
        self.readers = []
        self.dma_sem = None
        self.dma_cnt = 0

    def __getitem__(self, idx):
        return self.t[idx]


class Op:
    __slots__ = ("eng", "fn", "deps", "signal", "mile", "dma_buf", "dma_target", "is_dma")

    def __init__(self, eng, fn):
        self.eng = eng
        self.fn = fn
        self.deps = []
        self.signal = False
        self.mile = None
        self.is_dma = False
        self.dma_buf = None
        self.dma_target = None


class Sched:
    def __init__(self, nc):
        self.nc = nc
        self.ops = {e: [] for e in ENGS}
        self.bufs = []

    def buf(self, name, t=None):
        b = Buf(name, t)
        self.bufs.append(b)
        return b

    def barrier(self):
        lasts = []
        for e in ENGS:
            for o in reversed(self.ops[e]):
                if o.fn is not None and not o.is_dma:
                    lasts.append(o)
                    break
        dmas = {}
        for e in ENGS:
            for o in self.ops[e]:
                if o.is_dma:
                    dmas[id(o.dma_buf)] = o
        for e in ENGS:
            o = Op(e, None)
            for d in lasts:
                o.deps.append(d)
                d.signal = True
            o.deps.extend(dmas.values())
            self.ops[e].append(o)
        for b in self.bufs:
            b.last_w = None
            b.readers = []

    def op(self, eng, fn, reads=(), writes=(), dma_buf=None, nowaw=False):
        o = Op(eng, fn)
        deps = []
        for b in reads:
            if b.last_w is not None:
                deps.append(b.last_w)
        for b in writes:
            if b.last_w is not None and not (nowaw and b.last_w.is_dma and not b.readers):
                deps.append(b.last_w)
            last_by_eng = {}
            for r in b.readers:
                if r.is_dma or r.eng == "pool":
                    deps.append(r)
                else:
                    last_by_eng[r.eng] = r
            deps.extend(last_by_eng.values())
        if dma_buf is not None:
            o.is_dma = True
            o.dma_buf = dma_buf
            dma_buf.dma_cnt += 1
            o.dma_target = 16 * dma_buf.dma_cnt
        seen = set()
        for d in deps:
            if d is o or id(d) in seen:
                continue
            seen.add(id(d))
            if eng == "pe" and d.eng == "pe" and not d.is_dma:
                continue
            o.deps.append(d)
            if not d.is_dma:
                d.signal = True
        for b in reads:
            b.readers.append(o)
        for b in writes:
            b.last_w = o
            b.readers = []
        self.ops[eng].append(o)
        return o

    def emit(self, newsem):
        nc = self.nc
        eng_sems = {e: [] for e in ENGS}
        for e in ENGS:
            cnt = 0
            for o in self.ops[e]:
                if o.signal and not o.is_dma:
                    o.mile = cnt
                    cnt += 1
            for i in range((cnt + SEM_CAP - 1) // SEM_CAP):
                eng_sems[e].append(newsem("s_%s_%d" % (e, i)))
        for e in ENGS:
            for o in self.ops[e]:
                if o.is_dma and o.dma_buf.dma_sem is None:
                    o.dma_buf.dma_sem = newsem("d_" + o.dma_buf.name)

        def run_engine(e, eng):
            waited = {}
            for o in self.ops[e]:
                for d in o.deps:
                    if d.is_dma:
                        key = ("dma", id(d.dma_buf))
                        if waited.get(key, 0) >= d.dma_target:
                            continue
                        waited[key] = d.dma_target
                        eng.wait_ge(d.dma_buf.dma_sem, d.dma_target)
                    else:
                        key = ("eng", d.eng)
                        if waited.get(key, -1) >= d.mile:
                            continue
                        waited[key] = d.mile
                        eng.wait_ge(eng_sems[d.eng][d.mile // SEM_CAP], d.mile % SEM_CAP + 1)
                if o.fn is None:
                    continue
                ins = o.fn(eng)
                if o.is_dma:
                    ins.then_inc(o.dma_buf.dma_sem, 16)
                elif o.signal:
                    ins.then_inc(eng_sems[e][o.mile // SEM_CAP], 1)
            if e in ("sp", "pool"):
                for o in self.ops[e]:
                    if o.is_dma:
                        key = ("dma", id(o.dma_buf))
                        val = 16 * o.dma_buf.dma_cnt
                        if waited.get(key, 0) >= val:
                            continue
                        waited[key] = val
                        eng.wait_ge(o.dma_buf.dma_sem, val)

        with nc.Block() as block:
            @block.tensor
            def _(eng):
                run_engine("pe", eng)

            @block.scalar
            def _(eng):
                run_engine("act", eng)

            @block.vector
            def _(eng):
                run_engine("dve", eng)

            @block.gpsimd
            def _(eng):
                run_engine("pool", eng)

            @block.sync
            def _(eng):
                run_engine("sp", eng)


class Ring:
    def __init__(self, bufs):
        self.bufs = bufs
        self.i = 0

    def next(self):
        b = self.bufs[self.i % len(self.bufs)]
        self.i += 1
        return b


def build(L):
    NT = L // 512
    NKO = L // 128
    nc = bass.Bass("TRN2", target_bir_lowering=False)

    def din(name, shape, dt=F32):
        return nc.dram_tensor(name, list(shape), dt, kind="ExternalInput").ap()

    def dscr(name, shape, dt):
        return nc.dram_tensor(name, list(shape), dt).ap()

    xT = {"p": din("xpT", [D_MODEL, L]), "s": din("xsT", [D_MODEL, L])}
    xr = {"p": din("xp", [L, D_MODEL]), "s": din("xs", [L, D_MODEL])}
    xoT = din("xoT", [3, D_MODEL, L])
    w_in = din("w_in", [D_MODEL, D_IN])
    wlr_d = din("wlr", [5, D_MODEL, 16])
    wa2_d = din("wa2aug", [17, 5, 256])
    w_o = din("w_o", [D_MODEL, D_MODEL])
    w_ff1 = din("w_ff1", [D_MODEL, D_FF])
    w_ff2 = din("w_ff2", [D_FF, D_MODEL])
    lamv_d = din("lamv", [128, 4, 64])
    dng_d = din("dng", [128, 1])
    gng_d = din("gng", [128, 1])
    lngb_d = din("lngb", [128, 4, D_MODEL])
    tri_d = din("tri", [128, 6, 128])
    ident_d = din("ident", [128, 128])
    dtab_d = din("dtab", [128, 4, 512])
    biasBA_d = din("biasBA", [128, 4, 2, NKO + 1])
    biasO_d = din("biasO", [128, 4, NT, 3 * NKO])
    kaug_d = {"p": din("kaugp", [2, 4 * L], BF16), "s": din("kaugs", [2, L], BF16)}
    qaug_d = din("qaug", [2, 4, 2, 512], BF16)
    flags_d = din("flags", [128, 9])
    yout = {"p": nc.dram_tensor("yp", [L, D_MODEL], F32, kind="ExternalOutput").ap(),
            "s": nc.dram_tensor("ys", [L, D_MODEL], F32, kind="ExternalOutput").ap()}

    NK = {"p": 4 * L, "s": L}
    OWN0 = {"p": 3 * L, "s": 0}
    Kscr = {j: dscr("Kscr" + j, [4, 2, 64, NK[j]], BF16) for j in "ps"}
    Vscr = {j: dscr("Vscr" + j, [NK[j] // 128, 128, 512], BF16) for j in "ps"}
    Qscr = {j: dscr("Qscr" + j, [4, 2, 64, L], BF16) for j in "ps"}
    OFscr = {j: dscr("OFscr" + j, [4, 128, L], F32) for j in "ps"}
    CATscr = {j: dscr("CATscr" + j, [D_MODEL, L], BF16) for j in "ps"}
    X1scr = {j: dscr("X1scr" + j, [L, D_MODEL], F32) for j in "ps"}
    X1Tscr = {j: dscr("X1Tscr" + j, [D_MODEL, L], BF16) for j in "ps"}

    S = Sched(nc)
    uid = [0]

    with ExitStack() as top:
        def newsem(name):
            uid[0] += 1
            return top.enter_context(nc.semaphore("%s_%d" % (name, uid[0])))

        def sbt(es, name, shape, dt):
            uid[0] += 1
            return S.buf(name, es.enter_context(nc.sbuf_tensor("%s_%d" % (name, uid[0]), list(shape), dt)))

        def pst(es, name, shape, dt=F32):
            uid[0] += 1
            return S.buf(name, es.enter_context(nc.psum_tensor("%s_%d" % (name, uid[0]), list(shape), dt)))

        def ring(es, name, n, shape, dt):
            return Ring([sbt(es, "%s%d" % (name, i), shape, dt) for i in range(n)])

        def dma(q, out_ap, in_ap, reads, writes, dbuf, nowaw=False):
            S.op(q, lambda e: e.dma_start(out=out_ap, in_=in_ap), reads=reads, writes=writes, dma_buf=dbuf, nowaw=nowaw)

        def mm(out_b, out_ap, l_b, l_ap, r_b, r_ap, start, stop):
            S.op("pe", lambda e: e.matmul(out_ap, lhsT=l_ap, rhs=r_ap, start=start, stop=stop),
                 reads=[l_b, r_b], writes=[out_b])

        def act(out_b, out_ap, in_b, in_ap, func, bias=None, scale=1.0, extra_reads=()):
            if bias is None:
                S.op("act", lambda e: e.activation(out=out_ap, in_=in_ap, func=func, scale=scale),
                     reads=[in_b] + list(extra_reads), writes=[out_b])
            else:
                S.op("act", lambda e: e.activation(out=out_ap, in_=in_ap, func=func, bias=bias, scale=scale),
                     reads=[in_b] + list(extra_reads), writes=[out_b])

        def tt(eng, out_b, out_ap, a_b, a_ap, b_b, b_ap, op):
            S.op(eng, lambda e: e.tensor_tensor(out=out_ap, in0=a_ap, in1=b_ap, op=op),
                 reads=[a_b, b_b], writes=[out_b])

        def ts(eng, out_b, out_ap, a_b, a_ap, s1, s2, op0, op1=None, extra_reads=()):
            if op1 is None:
                S.op(eng, lambda e: e.tensor_scalar(out=out_ap, in0=a_ap, scalar1=s1, scalar2=None, op0=op0),
                     reads=[a_b] + list(extra_reads), writes=[out_b])
            else:
                S.op(eng, lambda e: e.tensor_scalar(out=out_ap, in0=a_ap, scalar1=s1, scalar2=s2, op0=op0, op1=op1),
                     reads=[a_b] + list(extra_reads), writes=[out_b])

        def stt(eng, out_b, out_ap, a_b, a_ap, sc, b_b, b_ap, op0, op1, extra_reads=()):
            S.op(eng, lambda e: e.scalar_tensor_tensor(out=out_ap, in0=a_ap, scalar=sc, in1=b_ap, op0=op0, op1=op1),
                 reads=[a_b, b_b] + list(extra_reads), writes=[out_b])

        def cp(eng, out_b, out_ap, in_b, in_ap):
            if eng == "act":
                S.op("act", lambda e: e.copy(out=out_ap, in_=in_ap), reads=[in_b], writes=[out_b])
            else:
                S.op(eng, lambda e: e.tensor_copy(out=out_ap, in_=in_ap), reads=[in_b], writes=[out_b])

        def memset(eng, b, ap, val):
            S.op(eng, lambda e: e.memset(ap, val), writes=[b])

        tri = sbt(top, "tri", [128, 6, 128], F32)
        identb = sbt(top, "identb", [128, 128], BF16)
        onesb = sbt(top, "onesb", [128, 128], BF16)
        onesf = sbt(top, "onesf", [128, 128], F32)
        negsix = sbt(top, "negsix", [128, 1], F32)
        lamv = sbt(top, "lamv", [128, 4, 64], F32)
        lamt = sbt(top, "lamt", [128, 64], F32)
        lams = sbt(top, "lams", [128, 4], F32)
        neglam = sbt(top, "neglam", [128, 1], F32)
        dgs = sbt(top, "dgs", [128, 1], F32)
        gng = sbt(top, "gng", [128, 1], F32)

        dma("sp", tri[:, :, :], tri_d[:, :, :], [], [tri], tri)
        dma("pool", identb[:, :], ident_d[:, :], [], [identb], identb)
        dma("sp", lamv[:, :, :], lamv_d[:, :, :], [], [lamv], lamv)
        dma("sp", dgs[:, :], dng_d[:, :], [], [dgs], dgs)
        dma("sp", gng[:, :], gng_d[:, :], [], [gng], gng)
        memset("dve", onesb, onesb[:, :], 1.0)
        memset("dve", onesf, onesf[:, :], 1.0)
        memset("dve", negsix, negsix[:, :], -1.0 / 16.0)
        ts("dve", dgs, dgs[:, :], dgs, dgs[:, :], 1.0 - LAM_INIT, None, ALU.mult)
        for i in range(2):
            tt("dve", lamt, lamt[:, :], lamv, lamv[:, 2 * i, :], lamv, lamv[:, 2 * i + 1, :], ALU.mult)
            S.op("dve", (lambda i: lambda e: e.reduce_sum(out=lams[:, i:i + 1], in_=lamt[:, :], axis=AX.X))(i),
                 reads=[lamt], writes=[lams])
        act(lams, lams[:, 2:4], lams, lams[:, 0:2], AF.Exp)
        tt("dve", neglam, neglam[:, :], lams, lams[:, 3:4], lams, lams[:, 2:3], ALU.subtract)
        ts("dve", neglam, neglam[:, :], neglam, neglam[:, :], -LAM_INIT, None, ALU.add)

        with ExitStack() as p1:
            Win = sbt(p1, "Win", [128, 8, D_IN], BF16)
            Wlr = sbt(p1, "Wlr", [128, 5, 8, 16], BF16)
            Wa2 = sbt(p1, "Wa2", [17, 5, 256], BF16)
            flags = sbt(p1, "flags", [128, 9], F32)
            w_in_v = w_in.rearrange("(c p) n -> p c n", p=128)
            for c in range(8):
                dma("pool", Win[:, c, :], w_in_v[:, c, :], [], [Win], Win, nowaw=True)
            for v in range(5):
                dma("pool", Wlr[:, v, :, :], wlr_d[v].rearrange("(c p) n -> p c n", p=128), [], [Wlr], Wlr, nowaw=True)
            dma("pool", Wa2[:, :, :], wa2_d[:, :, :], [], [Wa2], Wa2)
            dma("sp", flags[:, :], flags_d[:, :], [], [flags], flags)

            XTr = ring(p1, "XT", 2, [128, 8, 512], BF16)
            kstr = ring(p1, "kst", 2, [128, 512], BF16)
            vstr = ring(p1, "vst", 2, [128, 4, 512], BF16)
            LRaug = sbt(p1, "LRaug", [17, 512], BF16)
            t1r = ring(p1, "t1", 2, [128, 256], F32)
            lapr = ring(p1, "lap", 4, [128, 256], F32)
            e3 = sbt(p1, "e3", [128, 256], F32)
            khat = sbt(p1, "khat", [128, 256], BF16)
            vb = sbt(p1, "vb", [128, 512], BF16)
            dlast = sbt(p1, "dlast", [128, 2], F32)
            Sp = [sbt(p1, "Sp%d" % i, [128, 128], F32) for i in range(2)]
            Sbf = [sbt(p1, "Sbf%d" % i, [128, 128], BF16) for i in range(2)]
            SF = [sbt(p1, "SF%d" % i, [128, 128], F32) for i in range(2)]
            SB = [sbt(p1, "SB%d" % i, [128, 128], F32) for i in range(2)]
            gqT = sbt(p1, "gqT", [128, 2, 512], F32)
            gkT = sbt(p1, "gkT", [128, 2, 512], F32)
            E1 = [sbt(p1, "E1_%d" % i, [128, 128], F32) for i in range(2)]
            E2 = [sbt(p1, "E2_%d" % i, [128, 128], F32) for i in range(2)]
            qtl = [sbt(p1, "qtl%d" % i, [128, 128], BF16) for i in range(2)]
            ktl = [sbt(p1, "ktl%d" % i, [128, 128], BF16) for i in range(2)]
            amr = ring(p1, "am", 2, [128, 128], BF16)
            ofst = sbt(p1, "ofst", [128, 4, 512], F32)
            osum = sbt(p1, "osum", [128, 128], F32)
            sq = sbt(p1, "sq", [128, 128], F32)
            rstd = sbt(p1, "rstd", [128, 128], F32)
            srT = sbt(p1, "srT", [128, 4, 512], F32)
            srtmp = sbt(p1, "srtmp", [128, 512], F32)
            ogst = sbt(p1, "ogst", [128, 4, 512], BF16)

            pA = Ring([pst(p1, "pA%d" % i, [128, 512]) for i in range(2)])
            pG1 = pst(p1, "pG1", [128, 512])
            pG2 = pst(p1, "pG2", [128, 512])
            pE = pst(p1, "pE", [128, 512])
            pF = pst(p1, "pF", [128, 512])
            pO = pst(p1, "pO", [128, 512])
            pL = pst(p1, "pL", [128, 512])

            memset("dve", LRaug, LRaug[:, :], 1.0)

            def proj_fm(XT, col0, evac):
                pp = pA.next()
                for c in range(8):
                    mm(pp, pp[:, :], Win, Win[:, c, col0:col0 + 128], XT, XT[:, c, :], c == 0, c == 7)
                evac(pp)

            def gla_sub(XT, sub, variant, tri0, own, bwd, tok0, job, laps):
                cs = slice(sub * 128, (sub + 1) * 128)
                for c in range(8):
                    mm(pG1, pG1[:, :], XT, XT[:, c, cs], Win, Win[:, c, O_GK:O_GK + 512], c == 0, c == 7)
                for c in range(8):
                    mm(pG2, pG2[:, 0:256], XT, XT[:, c, cs], Win, Win[:, c, O_GK + 512:O_GK + 768], c == 0, c == 7)
                lap = laps[sub]
                if own:
                    for p in range(2):
                        bTb, bT = (pF, pF[:, 0:128]) if p == 0 else (pE, pE[:, 0:128])
                        mm(bTb, bT, lap, lap[:, p * 128:(p + 1) * 128], tri, tri[:, tri0, :], True, True)
                        act(E1[p], E1[p][:, :], bTb, bT, AF.Exp)
                        act(E2[p], E2[p][:, :], bTb, bT, AF.Exp, scale=-1.0)
                        stt("dve", qtl[p], qtl[p][:, :], gqT, gqT[:, p, cs], 0.125, E1[p], E1[p][:, :], ALU.mult, ALU.mult)
                        tt("dve", ktl[p], ktl[p][:, :], gkT, gkT[:, p, cs], E2[p], E2[p][:, :], ALU.mult)
                cp("act", vb, vb[:, 0:256], pG1, pG1[:, 256:512])
                cp("act", vb, vb[:, 256:512], pG2, pG2[:, 0:256])
                if own:
                    for h in range(4):
                        p, hh = h // 2, h % 2
                        rs = slice(hh * 64, (hh + 1) * 64)
                        if h % 2 == 0:
                            aTb, aT, oTb, oT, sS = pF, pF[:, 256:384], pO, pO[:, 0:128], pO[:, 128:256]
                        else:
                            aTb, aT, oTb, oT, sS = pL, pL[:, 0:128], pL, pL[:, 128:256], pL[:, 256:384]
                        mm(aTb, aT, ktl[p], ktl[p][rs, :], qtl[p], qtl[p][rs, :], True, True)
                        am = amr.next()
                        tt("dve", am, am[:, :], aTb, aT, tri, tri[:, tri0 + 2, :], ALU.mult)
                        mm(oTb, oT, Sbf[p], Sbf[p][rs, :], qtl[p], qtl[p][rs, :], True, False)
                        mm(oTb, oT, vb, vb[:, h * 128:(h + 1) * 128], am, am[:, :], False, True)
                        if not bwd:
                            cp("act", ofst, ofst[:, h, cs], oTb, oT)
                        else:
                            tt("dve", osum, osum[:, :], oTb, oT, ofst, ofst[:, h, cs], ALU.add)
                            tt("pool", sq, sq[:, :], osum, osum[:, :], osum, osum[:, :], ALU.mult)
                            mm(oTb, sS, onesf, onesf[:, :], sq, sq[:, :], True, True)
                            ts("dve", rstd, rstd[:, :], oTb, sS, 1.0 / 128.0, EPS, ALU.mult, ALU.add)
                            act(rstd, rstd[:, :], rstd, rstd[:, :], AF.Ln)
                            act(rstd, rstd[:, :], rstd, rstd[:, :], AF.Exp, scale=-0.5)
                            stt("dve", osum, osum[:, :], osum, osum[:, :], gng[:, 0:1], rstd, rstd[:, :], ALU.mult, ALU.mult,
                                extra_reads=[gng])
                            tt("dve", ogst, ogst[:, h, cs], osum, osum[:, :], srT, srT[:, h, cs], ALU.mult)
                mm(pE, pE[:, 0:256], tri, tri[:, tri0 + 1, :], lap, lap[:, :], True, True)
                act(e3, e3[:, :], pE, pE[:, 0:256], AF.Exp)
                tt("dve", khat, khat[:, :], pG1, pG1[:, 0:256], e3, e3[:, :], ALU.mult)
                for p in range(2):
                    mm(pF, pF[:, 384 + p:385 + p], lap, lap[:, p * 128:(p + 1) * 128], negsix, negsix[:, 0:1], True, True)
                act(dlast, dlast[:, 0:2], pF, pF[:, 384:386], AF.Exp)
                for p in range(2):
                    mm(pE, pE[:, 256:512], khat, khat[:, p * 128:(p + 1) * 128], vb, vb[:, p * 256:(p + 1) * 256], True, True)
                    for hh in range(2):
                        rs = slice(hh * 64, (hh + 1) * 64)
                        stt("dve", Sp[p], Sp[p][rs, :], Sp[p], Sp[p][rs, :], dlast[rs, p:p + 1],
                            pE, pE[rs, 256 + hh * 128:256 + (hh + 1) * 128], ALU.mult, ALU.add, extra_reads=[dlast])
                    if own:
                        cp("pool", Sbf[p], Sbf[p][:, :], Sp[p], Sp[p][:, :])

            def gate_z(variant):
                laps = []
                for sub in range(4):
                    cs = slice(sub * 128, (sub + 1) * 128)
                    zc = slice((sub % 2) * 256, (sub % 2) * 256 + 256)
                    mm(pL, pL[:, zc], LRaug, LRaug[0:17, cs], Wa2, Wa2[0:17, variant, :], True, True)
                    lap, t1 = lapr.next(), t1r.next()
                    act(t1, t1[:, :], pL, pL[:, zc], AF.Exp, scale=-1.0)
                    act(lap, lap[:, :], t1, t1[:, :], AF.Ln, bias=1.0)
                    laps.append(lap)
                return laps

            def lr_proj(XT, variant):
                for c in range(8):
                    mm(pL, pL[0:16, :], Wlr, Wlr[:, variant, c, :], XT, XT[:, c, :], c == 0, c == 7)
                cp("act", LRaug, LRaug[0:16, :], pL, pL[0:16, :])

            def kv_proj(XT, job, key0, part):
                for h in (range(4) if part == "k" else []):
                    def evac(pp, h=h):
                        kst = kstr.next()
                        cp("act", kst, kst[:, :], pp, pp[:, :])
                        dma("sp", Kscr[job][h].rearrange("c r k -> (c r) k")[:, key0:key0 + 512], kst[:, :], [kst], [], kst)
                    proj_fm(XT, O_DK + h * 128, evac)
                if part == "k":
                    return
                vst = vstr.next()
                for sub in range(4):
                    pp = pA.next()
                    for c in range(8):
                        mm(pp, pp[:, :], XT, XT[:, c, sub * 128:(sub + 1) * 128], Win, Win[:, c, O_DV:O_DV + 512], c == 0, c == 7)
                    cp("dve", vst, vst[:, sub, :], pp, pp[:, :])
                kt0 = key0 // 128
                dma("sp", Vscr[job][kt0:kt0 + 4].rearrange("k p n -> p k n"), vst[:, :, :], [vst], [], vst)

            def load_xT(src_ap, t0):
                XT = XTr.next()
                dma("pool", XT[:, :, :], src_ap.rearrange("(c p) t -> p c t", p=128)[:, :, t0:t0 + 512], [], [XT], XT)
                return XT

            for job in ("p", "s"):
                for p in range(2):
                    memset("dve", Sp[p], Sp[p][:, :], 0.0)
                    memset("dve", SF[p], SF[p][:, :], 0.0)
                    memset("dve", SB[p], SB[p][:, :], 0.0)
                if job == "p":
                    for slot in range(3):
                        for ti in range(NT):
                            XT = load_xT(xoT[slot], ti * 512)
                            lr_proj(XT, slot)
                            kv_proj(XT, job, slot * L + ti * 512, "k")
                            laps = gate_z(slot)
                            kv_proj(XT, job, slot * L + ti * 512, "v")
                            for sub in range(4):
                                gla_sub(XT, sub, slot, 0, False, False, 0, job, laps)
                        for p in range(2):
                            stt("dve", SF[p], SF[p][:, :], Sp[p], Sp[p][:, :], flags[:, slot:slot + 1], SF[p], SF[p][:, :],
                                ALU.mult, ALU.add, extra_reads=[flags])
                            stt("dve", SB[p], SB[p][:, :], Sp[p], Sp[p][:, :], flags[:, 3 + slot:4 + slot], SB[p], SB[p][:, :],
                                ALU.mult, ALU.add, extra_reads=[flags])
                            ts("dve", Sp[p], Sp[p][:, :], Sp[p], Sp[p][:, :], flags[:, 6 + slot:7 + slot], None, ALU.mult,
                               extra_reads=[flags])
                for p in range(2):
                    cp("dve", Sp[p], Sp[p][:, :], SF[p], SF[p][:, :])
                    cp("pool", Sbf[p], Sbf[p][:, :], SF[p], SF[p][:, :])
                for ti in range(NT):
                    XT = load_xT(xT[job], ti * 512)
                    lr_proj(XT, 3)
                    kv_proj(XT, job, OWN0[job] + ti * 512, "k")
                    laps = gate_z(3)
                    kv_proj(XT, job, OWN0[job] + ti * 512, "v")
                    for h in range(4):
                        def evacq(pp, h=h, ti=ti):
                            kst = kstr.next()
                            S.op("act", (lambda o, i: lambda e: e.mul(out=o, in_=i, mul=0.125))(kst[:, :], pp[:, :]),
                                 reads=[pp], writes=[kst])
                            dma("sp", Qscr[job][h].rearrange("c r k -> (c r) k")[:, ti * 512:(ti + 1) * 512], kst[:, :], [kst], [], kst)
                        proj_fm(XT, O_DQ + h * 128, evacq)
                    for p in range(2):
                        proj_fm(XT, O_GQ + p * 128, lambda pp, p=p: cp("act", gqT, gqT[:, p, :], pp, pp[:, :]))
                        proj_fm(XT, O_GK + p * 128, lambda pp, p=p: cp("act", gkT, gkT[:, p, :], pp, pp[:, :]))
                    for sub in range(4):
                        gla_sub(XT, sub, 3, 0, True, False, ti * 512, job, laps)
                    dma("sp", OFscr[job].rearrange("h p t -> p h t")[:, :, ti * 512:(ti + 1) * 512], ofst[:, :, :], [ofst], [], ofst)
                for p in range(2):
                    cp("dve", Sp[p], Sp[p][:, :], SB[p], SB[p][:, :])
                    cp("pool", Sbf[p], Sbf[p][:, :], SB[p], SB[p][:, :])
                for ti in reversed(range(NT)):
                    XT = load_xT(xT[job], ti * 512)
                    dma("sp", ofst[:, :, :], OFscr[job].rearrange("h p t -> p h t")[:, :, ti * 512:(ti + 1) * 512], [], [ofst], ofst)
                    lr_proj(XT, 4)
                    for p in range(2):
                        proj_fm(XT, O_GQ + p * 128, lambda pp, p=p: cp("act", gqT, gqT[:, p, :], pp, pp[:, :]))
                        proj_fm(XT, O_GK + p * 128, lambda pp, p=p: cp("act", gkT, gkT[:, p, :], pp, pp[:, :]))
                    laps = gate_z(4)
                    for h in range(4):
                        def evacr(pp, h=h):
                            act(srtmp, srtmp[:, :], pp, pp[:, :], AF.Exp, scale=-1.0)
                            ts("pool", srtmp, srtmp[:, :], srtmp, srtmp[:, :], 1.0, None, ALU.add)
                            S.op("dve", lambda e: e.reciprocal(out=srtmp[:, :], in_=srtmp[:, :]), reads=[srtmp], writes=[srtmp])
                            tt("dve", srT, srT[:, h, :], pp, pp[:, :], srtmp, srtmp[:, :], ALU.mult)
                        proj_fm(XT, O_GR + h * 128, evacr)
                    for sub in reversed(range(4)):
                        gla_sub(XT, sub, 4, 3, True, True, ti * 512, job, laps)
                    dma("sp", CATscr[job][512:1024, :].rearrange("(h p) t -> p h t", p=128)[:, :, ti * 512:(ti + 1) * 512],
                        ogst[:, :, :], [ogst], [], ogst)
            S.barrier()

        with ExitStack() as p2:
            NKmax = NK["p"]
            Kb = [sbt(p2, "Kb%d" % c, [66, NKmax], BF16) for c in range(2)]
            Vh = sbt(p2, "Vh", [128, NKmax // 128, 128], BF16)
            dtab = sbt(p2, "dtab", [128, 4, 512], F32)
            biasBA = sbt(p2, "biasBA", [128, 4, 2, NKO + 1], F32)
            biasO = sbt(p2, "biasO", [128, 4, NT, 3 * NKO], F32)
            PTr = ring(p2, "PT", 8, [128, 2, 512], BF16)
            tq1r = ring(p2, "tq1", 2, [128, 2, 512], F32)
            tq2r = ring(p2, "tq2", 2, [128, 2, 512], F32)
            sbr = ring(p2, "sbq", 3, [128, 2, 512], BF16)
            smr = ring(p2, "sm", 2, [128, 2, 512], F32)
            r0 = sbt(p2, "r0", [128, 512], F32)
            r1 = sbt(p2, "r1", [128, 512], F32)
            u0 = sbt(p2, "u0", [128, 512], F32)
            u1 = sbt(p2, "u1", [128, 512], F32)
            o0s = sbt(p2, "o0s", [128, 512], F32)
            o1s = sbt(p2, "o1s", [128, 512], F32)
            odst = ring(p2, "odst", 2, [128, 512], BF16)
            pS = Ring([pst(p2, "pS%d" % i, [128, 2, 512]) for i in range(2)])
            pOc = [pst(p2, "pOc%d" % c, [128, 512]) for c in range(2)]
            pDc = [pst(p2, "pDc%d" % c, [128, 512]) for c in range(2)]
            Qsets = [([sbt(p2, "QB%d_%d" % (c, i), [66, 512], BF16) for c in range(2)],
                      [sbt(p2, "QA%d_%d" % (c, i), [66, 512], BF16) for c in range(2)]) for i in range(2)]

            dma("sp", dtab[:, :, :], dtab_d[:, :, :], [], [dtab], dtab)
            dma("sp", biasBA[:, :, :, :], biasBA_d[:, :, :, :], [], [biasBA], biasBA)
            dma("sp", biasO[:, :, :, :], biasO_d[:, :, :, :], [], [biasO], biasO)

            def min_dist(job, qi, kt):
                own_kt0_ = OWN0[job] // 128
                t0 = qi * 512
                if kt >= own_kt0_:
                    s0 = (kt - own_kt0_) * 128
                    if s0 + 128 <= t0:
                        return t0 - (s0 + 127)
                    if s0 >= t0 + 512:
                        return s0 - (t0 + 511)
                    return 0
                slot, ktl = kt // NKO, kt % NKO
                best = None
                for j in range(4):
                    order = [(i, True) for i in range(j)] + [(i, False) for i in range(3, j, -1)]
                    i, isb = order[slot]
                    if isb:
                        d = (j * L + t0) - (i * L + 128 * ktl + 127)
                    else:
                        d = (i * L + L - 128 * ktl - 128) - (j * L + t0 + 511)
                    best = d if best is None else min(best, d)
                return best

            def active_tiles(job, h, qi):
                nkt_ = NK[job] // 128
                lim = (104.0 + 64.0) / SLOPES[h]
                dist = [min_dist(job, qi, kt) for kt in range(nkt_)]
                keep = [kt for kt in range(nkt_) if dist[kt] <= lim]
                extra = sorted((kt for kt in range(nkt_) if dist[kt] > lim), key=lambda kt: dist[kt])
                while len(keep) % 4 != 0 or len(keep) < 4:
                    keep.append(extra.pop(0))
                return sorted(keep)

            def load_q(job, h, qi):
                QBs, QAs = Qsets[qi % 2]
                for c in range(2):
                    dma("sp", QBs[c][0:64, :], Qscr[job][h, c, :, qi * 512:(qi + 1) * 512], [], [QBs[c]], QBs[c])
                    dma("sp", QAs[c][0:64, :], Qscr[job][h, c, :, qi * 512:(qi + 1) * 512], [], [QAs[c]], QAs[c])

            for job in ("p", "s"):
                nk = NK[job]
                nkt = nk // 128
                own_kt0 = OWN0[job] // 128
                for c in range(2):
                    dma("sp", Kb[c][64:66, 0:nk], kaug_d[job][:, :], [], [Kb[c]], Kb[c])
                for h in range(4):
                    for c in range(2):
                        dma("sp", Kb[c][0:64, 0:nk], Kscr[job][h, c], [], [Kb[c]], Kb[c])
                        for QBs, QAs in Qsets:
                            dma("sp", QBs[c][64:66, :], qaug_d[:, h, 0, :], [], [QBs[c]], QBs[c])
                            dma("sp", QAs[c][64:66, :], qaug_d[:, h, 1, :], [], [QAs[c]], QAs[c])
                    for k0 in range(0, nkt, 16):
                        k1 = min(nkt, k0 + 16)
                        dma("sp", Vh[:, k0:k1, :], Vscr[job][k0:k1, :, h * 128:(h + 1) * 128].rearrange("k p e -> p k e"),
                            [], [Vh], Vh, nowaw=(k0 > 0))
                    load_q(job, h, 0)
                    def make_tile(qi, job=job, h=h):
                        QB, QA = Qsets[qi % 2]
                        st = {}

                        act_kts = active_tiles(job, h, qi)
                        n_act = len(act_kts)

                        def qk(i, qi=qi, QB=QB, QA=QA, st=st, h=h, job=job, act_kts=act_kts):
                            kt = act_kts[i]
                            if kt >= own_kt0:
                                s0 = (kt - own_kt0) * 128
                                t0 = qi * 512
                                if s0 + 128 <= t0:
                                    cls, Qs, R = "B", QB, 66
                                    jx = (t0 - s0) // 128
                                    bias_ap, bias_b = biasBA[:, h, 0, jx:jx + 1], biasBA
                                elif s0 >= t0 + 512:
                                    cls, Qs, R = "A", QA, 66
                                    jx = (s0 - t0) // 128
                                    bias_ap, bias_b = biasBA[:, h, 1, jx:jx + 1], biasBA
                                else:
                                    cls, Qs, R = "D", QB, 64
                                    bias_ap, bias_b = (s0 - t0) // 128, None
                            else:
                                cls, Qs, R = "O", QB, 66
                                bias_ap, bias_b = biasO[:, h, qi, kt:kt + 1], biasO
                            ps_ = pS.next()
                            for c in range(2):
                                mm(ps_, ps_[:, c, :], Kb[c], Kb[c][0:R, kt * 128:(kt + 1) * 128], Qs[c], Qs[c][0:R, :], True, True)
                            PT = PTr.next()
                            if cls == "D":
                                sm = smr.next()
                                for c in range(2):
                                    stt("dve", sm, sm[:, c, :], dtab, dtab[:, bias_ap, :], -SLOPES[h], ps_, ps_[:, c, :],
                                        ALU.mult, ALU.add)
                                act(PT, PT[:, :, :], sm, sm[:, :, :], AF.Exp)
                            else:
                                act(PT, PT[:, :, :], ps_, ps_[:, :, :], AF.Exp, bias=bias_ap, extra_reads=[bias_b])
                            st[i] = PT

                        pend = []

                        def pv(i, st=st, pend=pend, act_kts=act_kts, n_act=n_act):
                            kt_v = act_kts[i]
                            kt = i
                            PT = st[kt]
                            first, last = kt == 0, kt == n_act - 1
                            for c in range(2):
                                mm(pOc[c], pOc[c][:, :], Vh, Vh[:, kt_v, :], PT, PT[:, c, :], first, last)
                            if kt % 4 == 1:
                                a, b = st.pop(kt - 1), st.pop(kt)
                                t1 = tq1r.next()
                                tt("dve", t1, t1[:, :, :], a, a[:, :, :], b, b[:, :, :], ALU.add)
                                st["t1"] = t1
                            if kt % 4 == 3:
                                c_, d = st.pop(kt - 1), st.pop(kt)
                                t1, t2, sb = st.pop("t1"), tq2r.next(), sbr.next()
                                tt("dve", t2, t2[:, :, :], c_, c_[:, :, :], d, d[:, :, :], ALU.add)
                                tt("dve", sb, sb[:, :, :], t1, t1[:, :, :], t2, t2[:, :, :], ALU.add)
                                pend.append((kt + 2, sb, kt == 3, last))
                            while pend and (pend[0][0] <= kt or last):
                                _, sb_, f_, l_ = pend.pop(0)
                                for c in range(2):
                                    mm(pDc[c], pDc[c][:, :], onesb, onesb[:, :], sb_, sb_[:, c, :], f_, l_)

                        LA = 3

                        def prologue():
                            for i in range(min(LA, n_act)):
                                qk(i)

                        def body(hook=None, hook_at=6):
                            if qi + 1 < NT:
                                load_q(job, h, qi + 1)
                            for i in range(n_act):
                                if i + LA < n_act:
                                    qk(i + LA)
                                pv(i)
                                if hook is not None and i == min(hook_at, n_act - 1):
                                    hook()

                        def epi_a():
                            cp("dve", o0s, o0s[:, :], pOc[0], pOc[0][:, :])
                            cp("dve", o1s, o1s[:, :], pOc[1], pOc[1][:, :])
                            S.op("dve", lambda e: e.reciprocal(out=r0[:, :], in_=pDc[0][:, :]), reads=[pDc[0]], writes=[r0])
                            S.op("dve", lambda e: e.reciprocal(out=r1[:, :], in_=pDc[1][:, :]), reads=[pDc[1]], writes=[r1])
                            tt("dve", u0, u0[:, :], o0s, o0s[:, :], r0, r0[:, :], ALU.mult)
                            tt("dve", u1, u1[:, :], o1s, o1s[:, :], r1, r1[:, :], ALU.mult)
                            stt("dve", u0, u0[:, :], u1, u1[:, :], neglam[:, 0:1], u0, u0[:, :], ALU.mult, ALU.add, extra_reads=[neglam])
                            tt("pool", u1, u1[:, :], u0, u0[:, :], u0, u0[:, :], ALU.mult)

                        def epi_b():
                            pq = pS.next()
                            mm(pq, pq[:, 0, :], onesf, onesf[:, :], u1, u1[:, :], True, True)
                            ts("dve", r0, r0[:, :], pq, pq[:, 0, :], 1.0 / 128.0, EPS, ALU.mult, ALU.add)
                            act(r0, r0[:, :], r0, r0[:, :], AF.Ln)
                            act(r0, r0[:, :], r0, r0[:, :], AF.Exp, scale=-0.5)
                            od = odst.next()
                            stt("dve", od, od[:, :], u0, u0[:, :], dgs[:, 0:1], r0, r0[:, :], ALU.mult, ALU.mult, extra_reads=[dgs])
                            dma("sp", CATscr[job][h * 128:(h + 1) * 128, qi * 512:(qi + 1) * 512], od[:, :], [od], [], od)

                        return prologue, body, epi_a, epi_b

                    tls = [make_tile(qi) for qi in range(NT)]
                    tls[0][0]()
                    hook = None
                    for qi in range(NT):
                        tls[qi][1](hook)
                        tls[qi][2]()
                        if qi + 1 < NT:
                            tls[qi + 1][0]()
                            hook = tls[qi][3]
                        else:
                            tls[qi][3]()
                            hook = None
            S.barrier()

        def ln_a(es_bufs, y1):
            stats, mv, rs_ = es_bufs
            for hf in range(2):
                S.op("dve", (lambda hf: lambda e: e.bn_stats(out=stats[:, hf * 6:(hf + 1) * 6], in_=y1[:, hf * 512:(hf + 1) * 512]))(hf),
                     reads=[y1], writes=[stats])
            S.op("dve", lambda e: e.bn_aggr(out=mv[:, :], in_=stats[:, :]), reads=[stats], writes=[mv])
            ts("dve", rs_, rs_[:, :], mv, mv[:, 1:2], EPS, None, ALU.add)
            act(rs_, rs_[:, :], rs_, rs_[:, :], AF.Ln)
            act(rs_, rs_[:, :], rs_, rs_[:, :], AF.Exp, scale=-0.5)

        def ln_b(es_bufs, y1, lngb, outb, out_ap):
            stats, mv, rs_ = es_bufs
            ts("dve", y1, y1[:, :], y1, y1[:, :], mv[:, 0:1], rs_[:, 0:1], ALU.subtract, ALU.mult, extra_reads=[mv, rs_])
            tt("dve", y1, y1[:, :], y1, y1[:, :], lngb, lngb[:, 0, :], ALU.mult)
            tt("dve", outb, out_ap, y1, y1[:, :], lngb, lngb[:, 1, :], ALU.add)

        with ExitStack() as p3:
            lnbr = Ring([(sbt(p3, "stats%d" % i, [128, 12], F32), sbt(p3, "mv%d" % i, [128, 2], F32),
                          sbt(p3, "rs%d" % i, [128, 1], F32)) for i in range(3)])
            with ExitStack() as p3a:
                Wo = sbt(p3a, "Wo", [128, 8, D_MODEL], BF16)
                lngb1 = sbt(p3a, "lngb1", [128, 2, D_MODEL], F32)
                dma("sp", lngb1[:, :, :], lngb_d[:, 0:2, :], [], [lngb1], lngb1)
                w_o_v = w_o.rearrange("(c p) n -> p c n", p=128)
                for c in range(8):
                    dma("pool", Wo[:, c, :], w_o_v[:, c, :], [], [Wo], Wo, nowaw=True)
                CTr = ring(p3a, "CT", 2, [128, 8, 512], BF16)
                xrr = ring(p3a, "xres", 5, [128, D_MODEL], F32)
                y1r = ring(p3a, "y1", 3, [128, D_MODEL], F32)
                x1r = ring(p3a, "x1", 3, [128, D_MODEL], F32)
                x1b = sbt(p3a, "x1b", [128, D_MODEL], BF16)
                x1Tr = ring(p3a, "x1T", 2, [128, 8, 128], BF16)
                pMr = Ring([[pst(p3a, "pM%d_%d" % (i, k), [128, 512]) for i in range(2)] for k in range(2)])
                pT = pst(p3a, "pT", [128, 8, 128], BF16)
                subs3a = [(job, ti, sub) for job in ("p", "s") for ti in range(NT) for sub in range(4)]
                st3a = {}

                xres_of = {}

                def load_xres(k):
                    job, ti, sub = subs3a[k]
                    tok0 = ti * 512 + sub * 128
                    xres = xrr.next()
                    dma("sp", xres[:, :], xr[job][tok0:tok0 + 128, :], [], [xres], xres)
                    xres_of[k] = xres

                def a3(k):
                    job, ti, sub = subs3a[k]
                    if sub == 0:
                        CT = CTr.next()
                        dma("sp", CT[:, :, :], CATscr[job].rearrange("(c p) t -> p c t", p=128)[:, :, ti * 512:(ti + 1) * 512],
                            [], [CT], CT)
                        st3a["CT"] = CT
                    CT = st3a["CT"]
                    tok0 = ti * 512 + sub * 128
                    if k + 2 < len(subs3a):
                        load_xres(k + 2)
                    xres = xres_of.pop(k)
                    y1 = y1r.next()
                    pM = pMr.next()
                    for hf in range(2):
                        for c in range(8):
                            mm(pM[hf], pM[hf][:, :], CT, CT[:, c, sub * 128:(sub + 1) * 128],
                               Wo, Wo[:, c, hf * 512:(hf + 1) * 512], c == 0, c == 7)
                        stt("dve", y1, y1[:, hf * 512:(hf + 1) * 512], xres, xres[:, hf * 512:(hf + 1) * 512], ALPHA,
                            pM[hf], pM[hf][:, :], ALU.mult, ALU.add)
                    lnb = lnbr.next()
                    ln_a(lnb, y1)
                    st3a[k] = (y1, lnb)

                st3b = {}

                def b3a(k):
                    job, ti, sub = subs3a[k]
                    tok0 = ti * 512 + sub * 128
                    y1, lnb = st3a.pop(k)
                    x1 = x1r.next()
                    ln_b(lnb, y1, lngb1, x1, x1[:, :])
                    dma("sp", X1scr[job][tok0:tok0 + 128, :], x1[:, :], [x1], [], x1)
                    st3b[k] = x1

                def b3b(k):
                    job, ti, sub = subs3a[k]
                    tok0 = ti * 512 + sub * 128
                    x1 = st3b.pop(k)
                    cp("act", x1b, x1b[:, :], x1, x1[:, :])
                    for c in range(8):
                        S.op("pe", (lambda c: lambda e: e.transpose(out=pT[:, c, :], in_=x1b[:, c * 128:(c + 1) * 128],
                                                                    identity=identb[:, :]))(c),
                             reads=[x1b, identb], writes=[pT])
                    x1T = x1Tr.next()
                    cp("act", x1T, x1T[:, :, :], pT, pT[:, :, :])
                    dma("sp", X1Tscr[job].rearrange("(c p) t -> p c t", p=128)[:, :, tok0:tok0 + 128], x1T[:, :, :],
                        [x1T], [], x1T)

                n3 = len(subs3a)
                load_xres(0)
                if n3 > 1:
                    load_xres(1)
                a3(0)
                if n3 > 1:
                    a3(1)
                b3a(0)
                for k in range(n3):
                    if k + 2 < n3:
                        a3(k + 2)
                    if k + 1 < n3:
                        b3a(k + 1)
                    b3b(k)
                S.barrier()

            with ExitStack() as p3b:
                W1 = sbt(p3b, "W1", [128, 8, D_FF], BF16)
                lngb2 = sbt(p3b, "lngb2", [128, 2, D_MODEL], F32)
                dma("sp", lngb2[:, :, :], lngb_d[:, 2:4, :], [], [lngb2], lngb2)
                W2 = sbt(p3b, "W2", [128, 32, D_MODEL], BF16)
                w1v = w_ff1.rearrange("(c p) n -> p c n", p=128)
                w2v = w_ff2.rearrange("(c p) n -> p c n", p=128)
                for c in range(8):
                    dma("pool", W1[:, c, :], w1v[:, c, :], [], [W1], W1, nowaw=True)
                for c in range(32):
                    dma("pool", W2[:, c, :], w2v[:, c, :], [], [W2], W2, nowaw=True)
                X1Tr = ring(p3b, "X1T", 1, [128, 8, 512], BF16)
                HT = sbt(p3b, "HT", [128, 32, 512], BF16)
                rlr = ring(p3b, "rl", 2, [128, 512], F32)
                x1rr = ring(p3b, "x1res", 1, [128, D_MODEL], F32)
                y2r = ring(p3b, "y2", 2, [128, D_MODEL], F32)
                yor = ring(p3b, "yo", 2, [128, D_MODEL], F32)
                pH = Ring([pst(p3b, "pH%d" % i, [128, 512]) for i in range(3)])
                pO2 = [pst(p3b, "pO2%d" % i, [128, 512]) for i in range(2)]
                pend3b = []
                for job in ("p", "s"):
                    for ti in range(NT):
                        X1T = X1Tr.next()
                        dma("sp", X1T[:, :, :], X1Tscr[job].rearrange("(c p) t -> p c t", p=128)[:, :, ti * 512:(ti + 1) * 512],
                            [], [X1T], X1T)
                        for f in range(32):
                            ph = pH.next()
                            for c in range(8):
                                mm(ph, ph[:, :], W1, W1[:, c, f * 128:(f + 1) * 128], X1T, X1T[:, c, :], c == 0, c == 7)
                            rl = rlr.next()
                            act(rl, rl[:, :], ph, ph[:, :], AF.Relu)
                            tt("pool", HT, HT[:, f, :], rl, rl[:, :], rl, rl[:, :], ALU.mult)
                        for sub in range(4):
                            tok0 = ti * 512 + sub * 128
                            x1res = x1rr.next()
                            dma("sp", x1res[:, :], X1scr[job][tok0:tok0 + 128, :], [], [x1res], x1res)
                            y2 = y2r.next()
                            for hf in range(2):
                                for f in range(32):
                                    mm(pO2[hf], pO2[hf][:, :], HT, HT[:, f, sub * 128:(sub + 1) * 128],
                                       W2, W2[:, f, hf * 512:(hf + 1) * 512], f == 0, f == 31)
                                stt("dve", y2, y2[:, hf * 512:(hf + 1) * 512], x1res, x1res[:, hf * 512:(hf + 1) * 512], ALPHA,
                                    pO2[hf], pO2[hf][:, :], ALU.mult, ALU.add)
                            lnb = lnbr.next()
                            ln_a(lnb, y2)
                            if pend3b:
                                pend3b.pop(0)()

                            def fin(lnb=lnb, y2=y2, job=job, tok0=tok0):
                                yo = yor.next()
                                ln_b(lnb, y2, lngb2, yo, yo[:, :])
                                dma("sp", yout[job][tok0:tok0 + 128, :], yo[:, :], [yo], [], yo)
                            pend3b.append(fin)
                while pend3b:
                    pend3b.pop(0)()
                S.barrier()

        S.emit(newsem)
    return nc


def _consts(L):
    NT, NKO = L // 512, L // 128
    k = np.arange(128)
    jj, ii = np.meshgrid(k, k, indexing="ij")
    tri = np.zeros((128, 6, 128), np.float32)
    tri[:, 0] = np.where(jj <= ii, -1.0 / 16, 0.0)
    tri[:, 1] = np.where(jj > ii, -1.0 / 16, 0.0)
    tri[:, 2] = np.where(jj <= ii, 1.0, 0.0)
    tri[:, 3] = np.where(jj >= ii, -1.0 / 16, 0.0)
    tri[:, 4] = np.where(jj < ii, -1.0 / 16, 0.0)
    tri[:, 5] = np.where(jj >= ii, 1.0, 0.0)
    ident = np.eye(128, dtype=np.float32)
    q = np.arange(512)
    dtab = np.zeros((128, 4, 512), np.float32)
    for j in range(4):
        dtab[:, j, :] = np.abs(q[None, :] - (128 * j + k[:, None]))
    biasBA = np.zeros((128, 4, 2, NKO + 1), np.float32)
    jv = np.arange(NKO + 1)
    for h in range(4):
        biasBA[:, h, 0, :] = -SLOPES[h] * (128.0 * jv[None, :] - k[:, None])
        biasBA[:, h, 1, :] = -SLOPES[h] * (128.0 * jv[None, :] + k[:, None])
    t_hi = (q // 2) * 2
    t_lo = q % 2
    qaug = np.zeros((2, 4, 2, 512), np.float32)
    for h in range(4):
        qaug[0, h, 0] = -SLOPES[h] * t_hi
        qaug[1, h, 0] = -SLOPES[h] * t_lo
        qaug[0, h, 1] = SLOPES[h] * t_hi
        qaug[1, h, 1] = SLOPES[h] * t_lo
    return tri, ident, dtab, biasBA, qaug


def make_in_maps(L, x_prompt, x_sample, w_in, w_o, lam_q1, lam_k1, lam_q2, lam_k2, diff_norm_g,
                 gla_wa2_f, gla_ba_f, gla_wa2_b, gla_ba_b, gla_norm_g, ln1_g, ln1_b, w_ff1, w_ff2, ln2_g, ln2_b):
    NT, NKO = L // 512, L // 128
    f = np.float32
    w_in0 = np.ascontiguousarray(w_in[0], dtype=f)
    tri, ident, dtab, biasBA, qaug = _consts(L)
    lamv = np.ascontiguousarray(np.broadcast_to(
        np.stack([lam_q1[0], lam_k1[0], lam_q2[0], lam_k2[0]])[None], (128, 4, 64)), dtype=f)
    lngb = np.ascontiguousarray(np.broadcast_to(
        np.stack([ln1_g[0], ln1_b[0], ln2_g[0], ln2_b[0]])[None], (128, 4, D_MODEL)), dtype=f)
    wa2aug_f = np.concatenate([gla_wa2_f[0], gla_ba_f[0][None]], 0)
    wa2aug_b = np.concatenate([gla_wa2_b[0], gla_ba_b[0][None]], 0)
    wlr_f = w_in0[:, O_LRF:O_LRB]
    wlr_b = w_in0[:, O_LRB:D_IN]
    shared = {
        "w_in": w_in0, "w_o": np.ascontiguousarray(w_o[0], dtype=f),
        "w_ff1": np.ascontiguousarray(w_ff1[0], dtype=f), "w_ff2": np.ascontiguousarray(w_ff2[0], dtype=f),
        "lamv": lamv, "dng": np.ascontiguousarray(diff_norm_g[0].reshape(128, 1), dtype=f),
        "gng": np.ascontiguousarray(gla_norm_g[0].reshape(128, 1), dtype=f), "lngb": lngb,
        "tri": tri, "ident": ident, "dtab": dtab, "biasBA": biasBA, "qaug": qaug.astype(ml_dtypes.bfloat16),
        "kaugs": np.ones((2, L), ml_dtypes.bfloat16),
    }
    k = np.arange(128)
    in_maps = []
    for c in range(8):
        b, j = c // 4, c % 4
        before = list(range(j))
        after = list(range(3, j, -1))
        slots = [(i, True) for i in before] + [(i, False) for i in after]
        nb = len(before)
        xo = np.zeros((3, D_MODEL, L), f)
        kaugp = np.ones((2, 4 * L), f)
        biasO = np.zeros((128, 4, NT, 3 * NKO), f)
        wlr = np.zeros((5, D_MODEL, 16), f)
        wa2 = np.zeros((17, 5, 256), f)
        for s, (i, isb) in enumerate(slots):
            seg = x_prompt[b, i * L:(i + 1) * L]
            pos = np.arange(i * L, (i + 1) * L)
            if not isb:
                seg = seg[::-1]
                pos = pos[::-1]
                kaugp[:, s * L:(s + 1) * L] = -1.0
            xo[s] = seg.T
            wlr[s] = wlr_f if isb else wlr_b
            wa2[:, s] = wa2aug_f if isb else wa2aug_b
            posm = pos.reshape(NKO, 128)
            for qi in range(NT):
                t0 = j * L + qi * 512
                d = (t0 - posm) if isb else (posm - t0)
                for h in range(4):
                    biasO[:, h, qi, s * NKO:(s + 1) * NKO] = (-SLOPES[h] * d.astype(f)).T
        wlr[3], wlr[4] = wlr_f, wlr_b
        wa2[:, 3], wa2[:, 4] = wa2aug_f, wa2aug_b
        flags = np.zeros((128, 9), f)
        for s in range(3):
            flags[:, s] = 1.0 if s == nb - 1 else 0.0
            flags[:, 3 + s] = 1.0 if (s == 2 and nb < 3) else 0.0
            flags[:, 6 + s] = 0.0 if s == nb - 1 else 1.0
        own = x_prompt[b, j * L:(j + 1) * L]
        m = dict(shared)
        m.update({
            "xpT": np.ascontiguousarray(own.T, dtype=f), "xp": np.ascontiguousarray(own, dtype=f),
            "xsT": np.ascontiguousarray(x_sample[c].T, dtype=f), "xs": np.ascontiguousarray(x_sample[c], dtype=f),
            "xoT": xo, "wlr": wlr, "wa2aug": wa2, "biasO": biasO, "kaugp": kaugp.astype(ml_dtypes.bfloat16), "flags": flags,
        })
        in_maps.append(m)
    return in_maps


_NC_CACHE = {}


def run(L, **inputs):
    inputs = {k_: np.asarray(v) for k_, v in inputs.items()}
    if L not in _NC_CACHE:
        _NC_CACHE[L] = build(L)
    nc = _NC_CACHE[L]
    in_maps = make_in_maps(L, **inputs)
    res = run_bass_kernel_spmd(nc, in_maps, core_ids=list(range(8)))
    yp = np.zeros((2, 4 * L, D_MODEL), np.float32)
    ys = np.zeros((8, L, D_MODEL), np.float32)
    for c in range(8):
        b, j = c // 4, c % 4
        yp[b, j * L:(j + 1) * L] = res.results[c]["yp"]
        ys[c] = res.results[c]["ys"]
    return yp, ys


def kernel(**inputs):
    return run(FULL_L, **inputs)
```

```python
import math
from contextlib import ExitStack

import numpy as np
import ml_dtypes
import concourse.bass as bass
import concourse.mybir as mybir
from concourse.bass_utils import run_bass_kernel_spmd

F32 = mybir.dt.float32
BF16 = mybir.dt.bfloat16
AF = mybir.ActivationFunctionType
ALU = mybir.AluOpType
AX = mybir.AxisListType

D_MODEL = 1024
D_IN = 3104
D_FF = 4096
O_DQ, O_DK, O_DV, O_GQ, O_GK, O_GV, O_GR, O_LRF, O_LRB = 0, 512, 1024, 1536, 1792, 2048, 2560, 3072, 3088
EPS = 1e-5
ALPHA = 2.0 ** 0.25
LAM_INIT = 0.8 - 0.6 * math.exp(0.0)
SLOPES = [2.0 ** (-8.0 * (h + 1) / 4) for h in range(4)]
FULL_L = 4096

SEM_CAP = 16000
ENGS = ("pe", "act", "dve", "pool", "sp")


class Buf:
    __slots__ = ("name", "t", "last_w", "readers", "dma_sem", "dma_cnt")

    def __init__(self, name, t=None):
        self.name = name
        self.t = t
        self.last_w = None
        self.readers = []
        self.dma_sem = None
        self.dma_cnt = 0

    def __getitem__(self, idx):
        return self.t[idx]


class Op:
    __slots__ = ("eng", "fn", "deps", "signal", "mile", "dma_buf", "dma_target", "is_dma")

    def __init__(self, eng, fn):
        self.eng = eng
        self.fn = fn
        self.deps = []
        self.signal = False
        self.mile = None
        self.is_dma = False
        self.dma_buf = None
        self.dma_target = None


class Sched:
    def __init__(self, nc):
        self.nc = nc
        self.ops = {e: [] for e in ENGS}
        self.bufs = []

    def buf(self, name, t=None):
        b = Buf(name, t)
        self.bufs.append(b)
        return b

    def barrier(self):
        lasts = []
        for e in ENGS:
            for o in reversed(self.ops[e]):
                if o.fn is not None and not o.is_dma:
                    lasts.append(o)
                    break
        dmas = {}
        for e in ENGS:
            for o in self.ops[e]:
                if o.is_dma:
                    dmas[id(o.dma_buf)] = o
        for e in ENGS:
            o = Op(e, None)
            for d in lasts:
                o.deps.append(d)
                d.signal = True
            o.deps.extend(dmas.values())
            self.ops[e].append(o)
        for b in self.bufs:
            b.last_w = None
            b.readers = []

    def op(self, eng, fn, reads=(), writes=(), dma_buf=None, nowaw=False):
        o = Op(eng, fn)
        deps = []
        for b in reads:
            if b.last_w is not None:
                deps.append(b.last_w)
        for b in writes:
            if b.last_w is not None and not (nowaw and b.last_w.is_dma and not b.readers):
                deps.append(b.last_w)
            last_by_eng = {}
            for r in b.readers:
                if r.is_dma or r.eng == "pool":
                    deps.append(r)
                else:
                    last_by_eng[r.eng] = r
            deps.extend(last_by_eng.values())
        if dma_buf is not None:
            o.is_dma = True
            o.dma_buf = dma_buf
            dma_buf.dma_cnt += 1
            o.dma_target = 16 * dma_buf.dma_cnt
        seen = set()
        for d in deps:
            if d is o or id(d) in seen:
                continue
            seen.add(id(d))
            if eng == "pe" and d.eng == "pe" and not d.is_dma:
                continue
            o.deps.append(d)
            if not d.is_dma:
                d.signal = True
        for b in reads:
            b.readers.append(o)
        for b in writes:
            b.last_w = o
            b.readers = []
        self.ops[eng].append(o)
        return o

    def emit(self, newsem):
        nc = self.nc
        eng_sems = {e: [] for e in ENGS}
        for e in ENGS:
            cnt = 0
            for o in self.ops[e]:
                if o.signal and not o.is_dma:
                    o.mile = cnt
                    cnt += 1
            for i in range((cnt + SEM_CAP - 1) // SEM_CAP):
                eng_sems[e].append(newsem("s_%s_%d" % (e, i)))
        for e in ENGS:
            for o in self.ops[e]:
                if o.is_dma and o.dma_buf.dma_sem is None:
                    o.dma_buf.dma_sem = newsem("d_" + o.dma_buf.name)

        def run_engine(e, eng):
            waited = {}
            for o in self.ops[e]:
                for d in o.deps:
                    if d.is_dma:
                        key = ("dma", id(d.dma_buf))
                        if waited.get(key, 0) >= d.dma_target:
                            continue
                        waited[key] = d.dma_target
                        eng.wait_ge(d.dma_buf.dma_sem, d.dma_target)
                    else:
                        key = ("eng", d.eng)
                        if waited.get(key, -1) >= d.mile:
                            continue
                        waited[key] = d.mile
                        eng.wait_ge(eng_sems[d.eng][d.mile // SEM_CAP], d.mile % SEM_CAP + 1)
                if o.fn is None:
                    continue
                ins = o.fn(eng)
                if o.is_dma:
                    ins.then_inc(o.dma_buf.dma_sem, 16)
                elif o.signal:
                    ins.then_inc(eng_sems[e][o.mile // SEM_CAP], 1)
            if e in ("sp", "pool"):
                for o in self.ops[e]:
                    if o.is_dma:
                        key = ("dma", id(o.dma_buf))
                        val = 16 * o.dma_buf.dma_cnt
                        if waited.get(key, 0) >= val:
                            continue
                        waited[key] = val
                        eng.wait_ge(o.dma_buf.dma_sem, val)

        with nc.Block() as block:
            @block.tensor
            def _(eng):
                run_engine("pe", eng)

            @block.scalar
            def _(eng):
                run_engine("act", eng)

            @block.vector
            def _(eng):
                run_engine("dve", eng)

            @block.gpsimd
            def _(eng):
                run_engine("pool", eng)

            @block.sync
            def _(eng):
                run_engine("sp", eng)


class Ring:
    def __init__(self, bufs):
        self.bufs = bufs
        self.i = 0

    def next(self):
        b = self.bufs[self.i % len(self.bufs)]
        self.i += 1
        return b


def build(L):
    NT = L // 512
    NKO = L // 128
    nc = bass.Bass("TRN2", target_bir_lowering=False)

    def din(name, shape, dt=F32):
        return nc.dram_tensor(name, list(shape), dt, kind="ExternalInput").ap()

    def dscr(name, shape, dt):
        return nc.dram_tensor(name, list(shape), dt).ap()

    xT = {"p": din("xpT", [D_MODEL, L]), "s": din("xsT", [D_MODEL, L])}
    xr = {"p": din("xp", [L, D_MODEL]), "s": din("xs", [L, D_MODEL])}
    xoT = din("xoT", [3, D_MODEL, L])
    w_in = din("w_in", [D_MODEL, D_IN])
    wlr_d = din("wlr", [5, D_MODEL, 16])
    wa2_d = din("wa2aug", [17, 5, 256])
    w_o = din("w_o", [D_MODEL, D_MODEL])
    w_ff1 = din("w_ff1", [D_MODEL, D_FF])
    w_ff2 = din("w_ff2", [D_FF, D_MODEL])
    lamv_d = din("lamv", [128, 4, 64])
    dng_d = din("dng", [128, 1])
    gng_d = din("gng", [128, 1])
    lngb_d = din("lngb", [128, 4, D_MODEL])
    tri_d = din("tri", [128, 6, 128])
    ident_d = din("ident", [128, 128])
    dtab_d = din("dtab", [128, 4, 512])
    biasBA_d = din("biasBA", [128, 4, 2, NKO + 1])
    biasO_d = din("biasO", [128, 4, NT, 3 * NKO])
    kaug_d = {"p": din("kaugp", [2, 4 * L], BF16), "s": din("kaugs", [2, L], BF16)}
    qaug_d = din("qaug", [2, 4, 2, 512], BF16)
    flags_d = din("flags", [128, 9])
    yout = {"p": nc.dram_tensor("yp", [L, D_MODEL], F32, kind="ExternalOutput").ap(),
            "s": nc.dram_tensor("ys", [L, D_MODEL], F32, kind="ExternalOutput").ap()}

    NK = {"p": 4 * L, "s": L}
    OWN0 = {"p": 3 * L, "s": 0}
    Kscr = {j: dscr("Kscr" + j, [4, 2, 64, NK[j]], BF16) for j in "ps"}
    Vscr = {j: dscr("Vscr" + j, [NK[j] // 128, 128, 512], BF16) for j in "ps"}
    Qscr = {j: dscr("Qscr" + j, [4, 2, 64, L], BF16) for j in "ps"}
    OFscr = {j: dscr("OFscr" + j, [4, 128, L], F32) for j in "ps"}
    CATscr = {j: dscr("CATscr" + j, [D_MODEL, L], BF16) for j in "ps"}
    X1scr = {j: dscr("X1scr" + j, [L, D_MODEL], F32) for j in "ps"}
    X1Tscr = {j: dscr("X1Tscr" + j, [D_MODEL, L], BF16) for j in "ps"}

    S = Sched(nc)
    uid = [0]

    with ExitStack() as top:
        def newsem(name):
            uid[0] += 1
            return top.enter_context(nc.semaphore("%s_%d" % (name, uid[0])))

        def sbt(es, name, shape, dt):
            uid[0] += 1
            return S.buf(name, es.enter_context(nc.sbuf_tensor("%s_%d" % (name, uid[0]), list(shape), dt)))

        def pst(es, name, shape, dt=F32):
            uid[0] += 1
            return S.buf(name, es.enter_context(nc.psum_tensor("%s_%d" % (name, uid[0]), list(shape), dt)))

        def ring(es, name, n, shape, dt):
            return Ring([sbt(es, "%s%d" % (name, i), shape, dt) for i in range(n)])

        def dma(q, out_ap, in_ap, reads, writes, dbuf, nowaw=False):
            S.op(q, lambda e: e.dma_start(out=out_ap, in_=in_ap), reads=reads, writes=writes, dma_buf=dbuf, nowaw=nowaw)

        def mm(out_b, out_ap, l_b, l_ap, r_b, r_ap, start, stop):
            S.op("pe", lambda e: e.matmul(out_ap, lhsT=l_ap, rhs=r_ap, start=start, stop=stop),
                 reads=[l_b, r_b], writes=[out_b])

        def act(out_b, out_ap, in_b, in_ap, func, bias=None, scale=1.0, extra_reads=()):
            if bias is None:
                S.op("act", lambda e: e.activation(out=out_ap, in_=in_ap, func=func, scale=scale),
                     reads=[in_b] + list(extra_reads), writes=[out_b])
            else:
                S.op("act", lambda e: e.activation(out=out_ap, in_=in_ap, func=func, bias=bias, scale=scale),
                     reads=[in_b] + list(extra_reads), writes=[out_b])

        def tt(eng, out_b, out_ap, a_b, a_ap, b_b, b_ap, op):
            S.op(eng, lambda e: e.tensor_tensor(out=out_ap, in0=a_ap, in1=b_ap, op=op),
                 reads=[a_b, b_b], writes=[out_b])

        def ts(eng, out_b, out_ap, a_b, a_ap, s1, s2, op0, op1=None, extra_reads=()):
            if op1 is None:
                S.op(eng, lambda e: e.tensor_scalar(out=out_ap, in0=a_ap, scalar1=s1, scalar2=None, op0=op0),
                     reads=[a_b] + list(extra_reads), writes=[out_b])
            else:
                S.op(eng, lambda e: e.tensor_scalar(out=out_ap, in0=a_ap, scalar1=s1, scalar2=s2, op0=op0, op1=op1),
                     reads=[a_b] + list(extra_reads), writes=[out_b])

        def stt(eng, out_b, out_ap, a_b, a_ap, sc, b_b, b_ap, op0, op1, extra_reads=()):
            S.op(eng, lambda e: e.scalar_tensor_tensor(out=out_ap, in0=a_ap, scalar=sc, in1=b_ap, op0=op0, op1=op1),
                 reads=[a_b, b_b] + list(extra_reads), writes=[out_b])

        def cp(eng, out_b, out_ap, in_b, in_ap):
            if eng == "act":
                S.op("act", lambda e: e.copy(out=out_ap, in_=in_ap), reads=[in_b], writes=[out_b])
            else:
                S.op(eng, lambda e: e.tensor_copy(out=out_ap, in_=in_ap), reads=[in_b], writes=[out_b])

        def memset(eng, b, ap, val):
            S.op(eng, lambda e: e.memset(ap, val), writes=[b])

        tri = sbt(top, "tri", [128, 6, 128], F32)
        identb = sbt(top, "identb", [128, 128], BF16)
        onesb = sbt(top, "onesb", [128, 128], BF16)
        onesf = sbt(top, "onesf", [128, 128], F32)
        negsix = sbt(top, "negsix", [128, 1], F32)
        lamv = sbt(top, "lamv", [128, 4, 64], F32)
        lamt = sbt(top, "lamt", [128, 64], F32)
        lams = sbt(top, "lams", [128, 4], F32)
        neglam = sbt(top, "neglam", [128, 1], F32)
        dgs = sbt(top, "dgs", [128, 1], F32)
        gng = sbt(top, "gng", [128, 1], F32)

        dma("sp", tri[:, :, :], tri_d[:, :, :], [], [tri], tri)
        dma("pool", identb[:, :], ident_d[:, :], [], [identb], identb)
        dma("sp", lamv[:, :, :], lamv_d[:, :, :], [], [lamv], lamv)
        dma("sp", dgs[:, :], dng_d[:, :], [], [dgs], dgs)
        dma("sp", gng[:, :], gng_d[:, :], [], [gng], gng)
        memset("dve", onesb, onesb[:, :], 1.0)
        memset("dve", onesf, onesf[:, :], 1.0)
        memset("dve", negsix, negsix[:, :], -1.0 / 16.0)
        ts("dve", dgs, dgs[:, :], dgs, dgs[:, :], 1.0 - LAM_INIT, None, ALU.mult)
        for i in range(2):
            tt("dve", lamt, lamt[:, :], lamv, lamv[:, 2 * i, :], lamv, lamv[:, 2 * i + 1, :], ALU.mult)
            S.op("dve", (lambda i: lambda e: e.reduce_sum(out=lams[:, i:i + 1], in_=lamt[:, :], axis=AX.X))(i),
                 reads=[lamt], writes=[lams])
        act(lams, lams[:, 2:4], lams, lams[:, 0:2], AF.Exp)
        tt("dve", neglam, neglam[:, :], lams, lams[:, 3:4], lams, lams[:, 2:3], ALU.subtract)
        ts("dve", neglam, neglam[:, :], neglam, neglam[:, :], -LAM_INIT, None, ALU.add)

        with ExitStack() as p1:
            Win = sbt(p1, "Win", [128, 8, D_IN], BF16)
            Wlr = sbt(p1, "Wlr", [128, 5, 8, 16], BF16)
            Wa2 = sbt(p1, "Wa2", [17, 5, 256], BF16)
            flags = sbt(p1, "flags", [128, 9], F32)
            w_in_v = w_in.rearrange("(c p) n -> p c n", p=128)
            for c in range(8):
                dma("pool", Win[:, c, :], w_in_v[:, c, :], [], [Win], Win, nowaw=True)
            for v in range(5):
                dma("pool", Wlr[:, v, :, :], wlr_d[v].rearrange("(c p) n -> p c n", p=128), [], [Wlr], Wlr, nowaw=True)
            dma("pool", Wa2[:, :, :], wa2_d[:, :, :], [], [Wa2], Wa2)
            dma("sp", flags[:, :], flags_d[:, :], [], [flags], flags)

            XTr = ring(p1, "XT", 2, [128, 8, 512], BF16)
            kstr = ring(p1, "kst", 2, [128, 512], BF16)
            vstr = ring(p1, "vst", 2, [128, 4, 512], BF16)
            LRaug = sbt(p1, "LRaug", [17, 512], BF16)
            t1r = ring(p1, "t1", 2, [128, 256], F32)
            lapr = ring(p1, "lap", 4, [128, 256], F32)
            e3 = sbt(p1, "e3", [128, 256], F32)
            khat = sbt(p1, "khat", [128, 256], BF16)
            vb = sbt(p1, "vb", [128, 512], BF16)
            dlast = sbt(p1, "dlast", [128, 2], F32)
            Sp = [sbt(p1, "Sp%d" % i, [128, 128], F32) for i in range(2)]
            Sbf = [sbt(p1, "Sbf%d" % i, [128, 128], BF16) for i in range(2)]
            SF = [sbt(p1, "SF%d" % i, [128, 128], F32) for i in range(2)]
            SB = [sbt(p1, "SB%d" % i, [128, 128], F32) for i in range(2)]
            gqT = sbt(p1, "gqT", [128, 2, 512], F32)
            gkT = sbt(p1, "gkT", [128, 2, 512], F32)
            E1 = [sbt(p1, "E1_%d" % i, [128, 128], F32) for i in range(2)]
            E2 = [sbt(p1, "E2_%d" % i, [128, 128], F32) for i in range(2)]
            qtl = [sbt(p1, "qtl%d" % i, [128, 128], BF16) for i in range(2)]
            ktl = [sbt(p1, "ktl%d" % i, [128, 128], BF16) for i in range(2)]
            amr = ring(p1, "am", 2, [128, 128], BF16)
            ofst = sbt(p1, "ofst", [128, 4, 512], F32)
            osum = sbt(p1, "osum", [128, 128], F32)
            sq = sbt(p1, "sq", [128, 128], F32)
            rstd = sbt(p1, "rstd", [128, 128], F32)
            srT = sbt(p1, "srT", [128, 4, 512], F32)
            srtmp = sbt(p1, "srtmp", [128, 512], F32)
            ogst = sbt(p1, "ogst", [128, 4, 512], BF16)

            pA = Ring([pst(p1, "pA%d" % i, [128, 512]) for i in range(2)])
            pG1 = pst(p1, "pG1", [128, 512])
            pG2 = pst(p1, "pG2", [128, 512])
            pE = pst(p1, "pE", [128, 512])
            pF = pst(p1, "pF", [128, 512])
            pO = pst(p1, "pO", [128, 512])
            pL = pst(p1, "pL", [128, 512])

            memset("dve", LRaug, LRaug[:, :], 1.0)

            def proj_fm(XT, col0, evac):
                pp = pA.next()
                for c in range(8):
                    mm(pp, pp[:, :], Win, Win[:, c, col0:col0 + 128], XT, XT[:, c, :], c == 0, c == 7)
                evac(pp)

            def gla_sub(XT, sub, variant, tri0, own, bwd, tok0, job, laps):
                cs = slice(sub * 128, (sub + 1) * 128)
                for c in range(8):
                    mm(pG1, pG1[:, :], XT, XT[:, c, cs], Win, Win[:, c, O_GK:O_GK + 512], c == 0, c == 7)
                for c in range(8):
                    mm(pG2, pG2[:, 0:256], XT, XT[:, c, cs], Win, Win[:, c, O_GK + 512:O_GK + 768], c == 0, c == 7)
                lap = laps[sub]
                if own:
                    for p in range(2):
                        bTb, bT = (pF, pF[:, 0:128]) if p == 0 else (pE, pE[:, 0:128])
                        mm(bTb, bT, lap, lap[:, p * 128:(p + 1) * 128], tri, tri[:, tri0, :], True, True)
                        act(E1[p], E1[p][:, :], bTb, bT, AF.Exp)
                        act(E2[p], E2[p][:, :], bTb, bT, AF.Exp, scale=-1.0)
                        stt("dve", qtl[p], qtl[p][:, :], gqT, gqT[:, p, cs], 0.125, E1[p], E1[p][:, :], ALU.mult, ALU.mult)
                        tt("dve", ktl[p], ktl[p][:, :], gkT, gkT[:, p, cs], E2[p], E2[p][:, :], ALU.mult)
                cp("act", vb, vb[:, 0:256], pG1, pG1[:, 256:512])
                cp("act", vb, vb[:, 256:512], pG2, pG2[:, 0:256])
                if own:
                    for h in range(4):
                        p, hh = h // 2, h % 2
                        rs = slice(hh * 64, (hh + 1) * 64)
                        if h % 2 == 0:
                            aTb, aT, oTb, oT, sS = pF, pF[:, 256:384], pO, pO[:, 0:128], pO[:, 128:256]
                        else:
                            aTb, aT, oTb, oT, sS = pL, pL[:, 0:128], pL, pL[:, 128:256], pL[:, 256:384]
                        mm(aTb, aT, ktl[p], ktl[p][rs, :], qtl[p], qtl[p][rs, :], True, True)
                        am = amr.next()
                        tt("dve", am, am[:, :], aTb, aT, tri, tri[:, tri0 + 2, :], ALU.mult)
                        mm(oTb, oT, Sbf[p], Sbf[p][rs, :], qtl[p], qtl[p][rs, :], True, False)
                        mm(oTb, oT, vb, vb[:, h * 128:(h + 1) * 128], am, am[:, :], False, True)
                        if not bwd:
                            cp("act", ofst, ofst[:, h, cs], oTb, oT)
                        else:
                            tt("dve", osum, osum[:, :], oTb, oT, ofst, ofst[:, h, cs], ALU.add)
                            tt("pool", sq, sq[:, :], osum, osum[:, :], osum, osum[:, :], ALU.mult)
                            mm(oTb, sS, onesf, onesf[:, :], sq, sq[:, :], True, True)
                            ts("dve", rstd, rstd[:, :], oTb, sS, 1.0 / 128.0, EPS, ALU.mult, ALU.add)
                            act(rstd, rstd[:, :], rstd, rstd[:, :], AF.Ln)
                            act(rstd, rstd[:, :], rstd, rstd[:, :], AF.Exp, scale=-0.5)
                            stt("dve", osum, osum[:, :], osum, osum[:, :], gng[:, 0:1], rstd, rstd[:, :], ALU.mult, ALU.mult,
                                extra_reads=[gng])
                            tt("dve", ogst, ogst[:, h, cs], osum, osum[:, :], srT, srT[:, h, cs], ALU.mult)
                mm(pE, pE[:, 0:256], tri, tri[:, tri0 + 1, :], lap, lap[:, :], True, True)
                act(e3, e3[:, :], pE, pE[:, 0:256], AF.Exp)
                tt("dve", khat, khat[:, :], pG1, pG1[:, 0:256], e3, e3[:, :], ALU.mult)
                for p in range(2):
                    mm(pF, pF[:, 384 + p:385 + p], lap, lap[:, p * 128:(p + 1) * 128], negsix, negsix[:, 0:1], True, True)
                act(dlast, dlast[:, 0:2], pF, pF[:, 384:386], AF.Exp)
                for p in range(2):
                    Ub = pE if p == 0 else pG2
                    mm(Ub, Ub[:, 256:512], khat, khat[:, p * 128:(p + 1) * 128], vb, vb[:, p * 256:(p + 1) * 256], True, True)
                    for hh in range(2):
                        rs = slice(hh * 64, (hh + 1) * 64)
                        stt("dve", Sp[p], Sp[p][rs, :], Sp[p], Sp[p][rs, :], dlast[rs, p:p + 1],
                            Ub, Ub[rs, 256 + hh * 128:256 + (hh + 1) * 128], ALU.mult, ALU.add, extra_reads=[dlast])
                    if own:
                        cp("pool", Sbf[p], Sbf[p][:, :], Sp[p], Sp[p][:, :])

            def gate_z(variant):
                laps = []
                for sub in range(4):
                    cs = slice(sub * 128, (sub + 1) * 128)
                    zc = slice((sub % 2) * 256, (sub % 2) * 256 + 256)
                    mm(pL, pL[:, zc], LRaug, LRaug[0:17, cs], Wa2, Wa2[0:17, variant, :], True, True)
                    lap, t1 = lapr.next(), t1r.next()
                    act(t1, t1[:, :], pL, pL[:, zc], AF.Exp, scale=-1.0)
                    act(lap, lap[:, :], t1, t1[:, :], AF.Ln, bias=1.0)
                    laps.append(lap)
                return laps

            def lr_proj(XT, variant):
                for c in range(8):
                    mm(pL, pL[0:16, :], Wlr, Wlr[:, variant, c, :], XT, XT[:, c, :], c == 0, c == 7)
                cp("act", LRaug, LRaug[0:16, :], pL, pL[0:16, :])

            def kv_proj(XT, job, key0, part):
                for h in (range(4) if part == "k" else []):
                    def evac(pp, h=h):
                        kst = kstr.next()
                        cp("act", kst, kst[:, :], pp, pp[:, :])
                        dma("sp", Kscr[job][h].rearrange("c r k -> (c r) k")[:, key0:key0 + 512], kst[:, :], [kst], [], kst)
                    proj_fm(XT, O_DK + h * 128, evac)
                if part == "k":
                    return
                vst = vstr.next()
                for sub in range(4):
                    pp = pA.next()
                    for c in range(8):
                        mm(pp, pp[:, :], XT, XT[:, c, sub * 128:(sub + 1) * 128], Win, Win[:, c, O_DV:O_DV + 512], c == 0, c == 7)
                    cp("dve", vst, vst[:, sub, :], pp, pp[:, :])
                kt0 = key0 // 128
                dma("sp", Vscr[job][kt0:kt0 + 4].rearrange("k p n -> p k n"), vst[:, :, :], [vst], [], vst)

            def load_xT(src_ap, t0):
                XT = XTr.next()
                dma("pool", XT[:, :, :], src_ap.rearrange("(c p) t -> p c t", p=128)[:, :, t0:t0 + 512], [], [XT], XT)
                return XT

            for job in ("p", "s"):
                for p in range(2):
                    memset("dve", Sp[p], Sp[p][:, :], 0.0)
                    memset("dve", SF[p], SF[p][:, :], 0.0)
                    memset("dve", SB[p], SB[p][:, :], 0.0)
                if job == "p":
                    for slot in range(3):
                        for ti in range(NT):
                            XT = load_xT(xoT[slot], ti * 512)
                            lr_proj(XT, slot)
                            kv_proj(XT, job, slot * L + ti * 512, "k")
                            laps = gate_z(slot)
                            kv_proj(XT, job, slot * L + ti * 512, "v")
                            for sub in range(4):
                                gla_sub(XT, sub, slot, 0, False, False, 0, job, laps)
                        for p in range(2):
                            stt("dve", SF[p], SF[p][:, :], Sp[p], Sp[p][:, :], flags[:, slot:slot + 1], SF[p], SF[p][:, :],
                                ALU.mult, ALU.add, extra_reads=[flags])
                            stt("dve", SB[p], SB[p][:, :], Sp[p], Sp[p][:, :], flags[:, 3 + slot:4 + slot], SB[p], SB[p][:, :],
                                ALU.mult, ALU.add, extra_reads=[flags])
                            ts("dve", Sp[p], Sp[p][:, :], Sp[p], Sp[p][:, :], flags[:, 6 + slot:7 + slot], None, ALU.mult,
                               extra_reads=[flags])
                for p in range(2):
                    cp("dve", Sp[p], Sp[p][:, :], SF[p], SF[p][:, :])
                    cp("pool", Sbf[p], Sbf[p][:, :], SF[p], SF[p][:, :])
                for ti in range(NT):
                    XT = load_xT(xT[job], ti * 512)
                    lr_proj(XT, 3)
                    kv_proj(XT, job, OWN0[job] + ti * 512, "k")
                    laps = gate_z(3)
                    kv_proj(XT, job, OWN0[job] + ti * 512, "v")
                    for h in range(4):
                        def evacq(pp, h=h, ti=ti):
                            kst = kstr.next()
                            S.op("act", (lambda o, i: lambda e: e.mul(out=o, in_=i, mul=0.125))(kst[:, :], pp[:, :]),
                                 reads=[pp], writes=[kst])
                            dma("sp", Qscr[job][h].rearrange("c r k -> (c r) k")[:, ti * 512:(ti + 1) * 512], kst[:, :], [kst], [], kst)
                        proj_fm(XT, O_DQ + h * 128, evacq)
                    for p in range(2):
                        proj_fm(XT, O_GQ + p * 128, lambda pp, p=p: cp("act", gqT, gqT[:, p, :], pp, pp[:, :]))
                        proj_fm(XT, O_GK + p * 128, lambda pp, p=p: cp("act", gkT, gkT[:, p, :], pp, pp[:, :]))
                    for sub in range(4):
                        gla_sub(XT, sub, 3, 0, True, False, ti * 512, job, laps)
                    dma("sp", OFscr[job].rearrange("h p t -> p h t")[:, :, ti * 512:(ti + 1) * 512], ofst[:, :, :], [ofst], [], ofst)
                for p in range(2):
                    cp("dve", Sp[p], Sp[p][:, :], SB[p], SB[p][:, :])
                    cp("pool", Sbf[p], Sbf[p][:, :], SB[p], SB[p][:, :])
                for ti in reversed(range(NT)):
                    XT = load_xT(xT[job], ti * 512)
                    dma("sp", ofst[:, :, :], OFscr[job].rearrange("h p t -> p h t")[:, :, ti * 512:(ti + 1) * 512], [], [ofst], ofst)
                    lr_proj(XT, 4)
                    for p in range(2):
                        proj_fm(XT, O_GQ + p * 128, lambda pp, p=p: cp("act", gqT, gqT[:, p, :], pp, pp[:, :]))
                        proj_fm(XT, O_GK + p * 128, lambda pp, p=p: cp("act", gkT, gkT[:, p, :], pp, pp[:, :]))
                    laps = gate_z(4)
                    for h in range(4):
                        def evacr(pp, h=h):
                            act(srtmp, srtmp[:, :], pp, pp[:, :], AF.Exp, scale=-1.0)
                            ts("pool", srtmp, srtmp[:, :], srtmp, srtmp[:, :], 1.0, None, ALU.add)
                            S.op("dve", lambda e: e.reciprocal(out=srtmp[:, :], in_=srtmp[:, :]), reads=[srtmp], writes=[srtmp])
                            tt("dve", srT, srT[:, h, :], pp, pp[:, :], srtmp, srtmp[:, :], ALU.mult)
                        proj_fm(XT, O_GR + h * 128, evacr)
                    for sub in reversed(range(4)):
                        gla_sub(XT, sub, 4, 3, True, True, ti * 512, job, laps)
                    dma("sp", CATscr[job][512:1024, :].rearrange("(h p) t -> p h t", p=128)[:, :, ti * 512:(ti + 1) * 512],
                        ogst[:, :, :], [ogst], [], ogst)
            S.barrier()

        with ExitStack() as p2:
            NKmax = NK["p"]
            Kb = [sbt(p2, "Kb%d" % c, [66, NKmax], BF16) for c in range(2)]
            Vh = sbt(p2, "Vh", [128, NKmax // 128, 128], BF16)
            dtab = sbt(p2, "dtab", [128, 4, 512], F32)
            biasBA = sbt(p2, "biasBA", [128, 4, 2, NKO + 1], F32)
            biasO = sbt(p2, "biasO", [128, 4, NT, 3 * NKO], F32)
            PTr = ring(p2, "PT", 8, [128, 2, 512], BF16)
            tq1r = ring(p2, "tq1", 2, [128, 2, 512], F32)
            tq2r = ring(p2, "tq2", 2, [128, 2, 512], F32)
            sbr = ring(p2, "sbq", 3, [128, 2, 512], BF16)
            smr = ring(p2, "sm", 2, [128, 2, 512], F32)
            r0 = sbt(p2, "r0", [128, 512], F32)
            r1 = sbt(p2, "r1", [128, 512], F32)
            u0 = sbt(p2, "u0", [128, 512], F32)
            u1 = sbt(p2, "u1", [128, 512], F32)
            o0s = sbt(p2, "o0s", [128, 512], F32)
            o1s = sbt(p2, "o1s", [128, 512], F32)
            odst = ring(p2, "odst", 2, [128, 512], BF16)
            pS = Ring([pst(p2, "pS%d" % i, [128, 2, 512]) for i in range(2)])
            pOc = [pst(p2, "pOc%d" % c, [128, 512]) for c in range(2)]
            pDc = [pst(p2, "pDc%d" % c, [128, 512]) for c in range(2)]
            Qsets = [([sbt(p2, "QB%d_%d" % (c, i), [66, 512], BF16) for c in range(2)],
                      [sbt(p2, "QA%d_%d" % (c, i), [66, 512], BF16) for c in range(2)]) for i in range(2)]

            dma("sp", dtab[:, :, :], dtab_d[:, :, :], [], [dtab], dtab)
            dma("sp", biasBA[:, :, :, :], biasBA_d[:, :, :, :], [], [biasBA], biasBA)
            dma("sp", biasO[:, :, :, :], biasO_d[:, :, :, :], [], [biasO], biasO)

            def min_dist(job, qi, kt):
                own_kt0_ = OWN0[job] // 128
                t0 = qi * 512
                if kt >= own_kt0_:
                    s0 = (kt - own_kt0_) * 128
                    if s0 + 128 <= t0:
                        return t0 - (s0 + 127)
                    if s0 >= t0 + 512:
                        return s0 - (t0 + 511)
                    return 0
                slot, ktl = kt // NKO, kt % NKO
                best = None
                for j in range(4):
                    order = [(i, True) for i in range(j)] + [(i, False) for i in range(3, j, -1)]
                    i, isb = order[slot]
                    if isb:
                        d = (j * L + t0) - (i * L + 128 * ktl + 127)
                    else:
                        d = (i * L + L - 128 * ktl - 128) - (j * L + t0 + 511)
                    best = d if best is None else min(best, d)
                return best

            def active_tiles(job, h, qi):
                nkt_ = NK[job] // 128
                lim = (104.0 + 64.0) / SLOPES[h]
                dist = [min_dist(job, qi, kt) for kt in range(nkt_)]
                keep = [kt for kt in range(nkt_) if dist[kt] <= lim]
                extra = sorted((kt for kt in range(nkt_) if dist[kt] > lim), key=lambda kt: dist[kt])
                while len(keep) % 4 != 0 or len(keep) < 4:
                    keep.append(extra.pop(0))
                return sorted(keep)

            def load_q(job, h, qi):
                QBs, QAs = Qsets[qi % 2]
                for c in range(2):
                    dma("sp", QBs[c][0:64, :], Qscr[job][h, c, :, qi * 512:(qi + 1) * 512], [], [QBs[c]], QBs[c])
                    dma("sp", QAs[c][0:64, :], Qscr[job][h, c, :, qi * 512:(qi + 1) * 512], [], [QAs[c]], QAs[c])

            for job in ("p", "s"):
                nk = NK[job]
                nkt = nk // 128
                own_kt0 = OWN0[job] // 128
                for c in range(2):
                    dma("sp", Kb[c][64:66, 0:nk], kaug_d[job][:, :], [], [Kb[c]], Kb[c])
                for h in range(4):
                    for c in range(2):
                        dma("sp", Kb[c][0:64, 0:nk], Kscr[job][h, c], [], [Kb[c]], Kb[c])
                        for QBs, QAs in Qsets:
                            dma("sp", QBs[c][64:66, :], qaug_d[:, h, 0, :], [], [QBs[c]], QBs[c])
                            dma("sp", QAs[c][64:66, :], qaug_d[:, h, 1, :], [], [QAs[c]], QAs[c])
                    for k0 in range(0, nkt, 16):
                        k1 = min(nkt, k0 + 16)
                        dma("sp", Vh[:, k0:k1, :], Vscr[job][k0:k1, :, h * 128:(h + 1) * 128].rearrange("k p e -> p k e"),
                            [], [Vh], Vh, nowaw=(k0 > 0))
                    load_q(job, h, 0)
                    def make_tile(qi, job=job, h=h):
                        QB, QA = Qsets[qi % 2]
                        st = {}

                        act_kts = active_tiles(job, h, qi)
                        n_act = len(act_kts)

                        def qk(i, qi=qi, QB=QB, QA=QA, st=st, h=h, job=job, act_kts=act_kts):
                            kt = act_kts[i]
                            if kt >= own_kt0:
                                s0 = (kt - own_kt0) * 128
                                t0 = qi * 512
                                if s0 + 128 <= t0:
                                    cls, Qs, R = "B", QB, 66
                                    jx = (t0 - s0) // 128
                                    bias_ap, bias_b = biasBA[:, h, 0, jx:jx + 1], biasBA
                                elif s0 >= t0 + 512:
                                    cls, Qs, R = "A", QA, 66
                                    jx = (s0 - t0) // 128
                                    bias_ap, bias_b = biasBA[:, h, 1, jx:jx + 1], biasBA
                                else:
                                    cls, Qs, R = "D", QB, 64
                                    bias_ap, bias_b = (s0 - t0) // 128, None
                            else:
                                cls, Qs, R = "O", QB, 66
                                bias_ap, bias_b = biasO[:, h, qi, kt:kt + 1], biasO
                            ps_ = pS.next()
                            for c in range(2):
                                mm(ps_, ps_[:, c, :], Kb[c], Kb[c][0:R, kt * 128:(kt + 1) * 128], Qs[c], Qs[c][0:R, :], True, True)
                            PT = PTr.next()
                            if cls == "D":
                                sm = smr.next()
                                for c in range(2):
                                    stt("dve", sm, sm[:, c, :], dtab, dtab[:, bias_ap, :], -SLOPES[h], ps_, ps_[:, c, :],
                                        ALU.mult, ALU.add)
                                act(PT, PT[:, :, :], sm, sm[:, :, :], AF.Exp)
                            else:
                                act(PT, PT[:, :, :], ps_, ps_[:, :, :], AF.Exp, bias=bias_ap, extra_reads=[bias_b])
                            st[i] = PT

                        pend = []

                        def pv(i, st=st, pend=pend, act_kts=act_kts, n_act=n_act):
                            kt_v = act_kts[i]
                            kt = i
                            PT = st[kt]
                            first, last = kt == 0, kt == n_act - 1
                            for c in range(2):
                                mm(pOc[c], pOc[c][:, :], Vh, Vh[:, kt_v, :], PT, PT[:, c, :], first, last)
                            if kt % 4 == 1:
                                a, b = st.pop(kt - 1), st.pop(kt)
                                t1 = tq1r.next()
                                tt("dve", t1, t1[:, :, :], a, a[:, :, :], b, b[:, :, :], ALU.add)
                                st["t1"] = t1
                            if kt % 4 == 3:
                                c_, d = st.pop(kt - 1), st.pop(kt)
                                t1, t2, sb = st.pop("t1"), tq2r.next(), sbr.next()
                                tt("dve", t2, t2[:, :, :], c_, c_[:, :, :], d, d[:, :, :], ALU.add)
                                tt("dve", sb, sb[:, :, :], t1, t1[:, :, :], t2, t2[:, :, :], ALU.add)
                                pend.append((kt + 2, sb, kt == 3, last))
                            while pend and (pend[0][0] <= kt or last):
                                _, sb_, f_, l_ = pend.pop(0)
                                for c in range(2):
                                    mm(pDc[c], pDc[c][:, :], onesb, onesb[:, :], sb_, sb_[:, c, :], f_, l_)

                        LA = 3

                        def prologue():
                            for i in range(min(LA, n_act)):
                                qk(i)

                        def body(hook=None, hook_at=6):
                            if qi + 1 < NT:
                                load_q(job, h, qi + 1)
                            for i in range(n_act):
                                if i + LA < n_act:
                                    qk(i + LA)
                                pv(i)
                                if hook is not None and i == min(hook_at, n_act - 1):
                                    hook()

                        def epi_a():
                            cp("dve", o0s, o0s[:, :], pOc[0], pOc[0][:, :])
                            cp("dve", o1s, o1s[:, :], pOc[1], pOc[1][:, :])
                            S.op("dve", lambda e: e.reciprocal(out=r0[:, :], in_=pDc[0][:, :]), reads=[pDc[0]], writes=[r0])
                            S.op("dve", lambda e: e.reciprocal(out=r1[:, :], in_=pDc[1][:, :]), reads=[pDc[1]], writes=[r1])
                            tt("dve", u0, u0[:, :], o0s, o0s[:, :], r0, r0[:, :], ALU.mult)
                            tt("dve", u1, u1[:, :], o1s, o1s[:, :], r1, r1[:, :], ALU.mult)
                            stt("dve", u0, u0[:, :], u1, u1[:, :], neglam[:, 0:1], u0, u0[:, :], ALU.mult, ALU.add, extra_reads=[neglam])
                            tt("pool", u1, u1[:, :], u0, u0[:, :], u0, u0[:, :], ALU.mult)

                        def epi_b():
                            pq = pS.next()
                            mm(pq, pq[:, 0, :], onesf, onesf[:, :], u1, u1[:, :], True, True)
                            ts("dve", r0, r0[:, :], pq, pq[:, 0, :], 1.0 / 128.0, EPS, ALU.mult, ALU.add)
                            act(r0, r0[:, :], r0, r0[:, :], AF.Ln)
                            act(r0, r0[:, :], r0, r0[:, :], AF.Exp, scale=-0.5)
                            od = odst.next()
                            stt("dve", od, od[:, :], u0, u0[:, :], dgs[:, 0:1], r0, r0[:, :], ALU.mult, ALU.mult, extra_reads=[dgs])
                            dma("sp", CATscr[job][h * 128:(h + 1) * 128, qi * 512:(qi + 1) * 512], od[:, :], [od], [], od)

                        return prologue, body, epi_a, epi_b

                    tls = [make_tile(qi) for qi in range(NT)]
                    tls[0][0]()
                    hook = None
                    for qi in range(NT):
                        tls[qi][1](hook)
                        tls[qi][2]()
                        if qi + 1 < NT:
                            tls[qi + 1][0]()
                            hook = tls[qi][3]
                        else:
                            tls[qi][3]()
                            hook = None
            S.barrier()

        def ln_a(es_bufs, y1):
            stats, mv, rs_ = es_bufs
            for hf in range(2):
                S.op("dve", (lambda hf: lambda e: e.bn_stats(out=stats[:, hf * 6:(hf + 1) * 6], in_=y1[:, hf * 512:(hf + 1) * 512]))(hf),
                     reads=[y1], writes=[stats])
            S.op("dve", lambda e: e.bn_aggr(out=mv[:, :], in_=stats[:, :]), reads=[stats], writes=[mv])
            ts("dve", rs_, rs_[:, :], mv, mv[:, 1:2], EPS, None, ALU.add)
            act(rs_, rs_[:, :], rs_, rs_[:, :], AF.Ln)
            act(rs_, rs_[:, :], rs_, rs_[:, :], AF.Exp, scale=-0.5)

        def ln_b(es_bufs, y1, lngb, outb, out_ap):
            stats, mv, rs_ = es_bufs
            ts("dve", y1, y1[:, :], y1, y1[:, :], mv[:, 0:1], rs_[:, 0:1], ALU.subtract, ALU.mult, extra_reads=[mv, rs_])
            tt("dve", y1, y1[:, :], y1, y1[:, :], lngb, lngb[:, 0, :], ALU.mult)
            tt("dve", outb, out_ap, y1, y1[:, :], lngb, lngb[:, 1, :], ALU.add)

        with ExitStack() as p3:
            lnbr = Ring([(sbt(p3, "stats%d" % i, [128, 12], F32), sbt(p3, "mv%d" % i, [128, 2], F32),
                          sbt(p3, "rs%d" % i, [128, 1], F32)) for i in range(3)])
            with ExitStack() as p3a:
                Wo = sbt(p3a, "Wo", [128, 8, D_MODEL], BF16)
                lngb1 = sbt(p3a, "lngb1", [128, 2, D_MODEL], F32)
                dma("sp", lngb1[:, :, :], lngb_d[:, 0:2, :], [], [lngb1], lngb1)
                w_o_v = w_o.rearrange("(c p) n -> p c n", p=128)
                for c in range(8):
                    dma("pool", Wo[:, c, :], w_o_v[:, c, :], [], [Wo], Wo, nowaw=True)
                CTr = ring(p3a, "CT", 2, [128, 8, 512], BF16)
                xrr = ring(p3a, "xres", 5, [128, D_MODEL], F32)
                y1r = ring(p3a, "y1", 3, [128, D_MODEL], F32)
                x1r = ring(p3a, "x1", 3, [128, D_MODEL], F32)
                x1b = sbt(p3a, "x1b", [128, D_MODEL], BF16)
                x1Tr = ring(p3a, "x1T", 2, [128, 8, 128], BF16)
                pMr = Ring([[pst(p3a, "pM%d_%d" % (i, k), [128, 512]) for i in range(2)] for k in range(2)])
                pT = pst(p3a, "pT", [128, 8, 128], BF16)
                subs3a = [(job, ti, sub) for job in ("p", "s") for ti in range(NT) for sub in range(4)]
                st3a = {}

                xres_of = {}

                def load_xres(k):
                    job, ti, sub = subs3a[k]
                    tok0 = ti * 512 + sub * 128
                    xres = xrr.next()
                    dma("sp", xres[:, :], xr[job][tok0:tok0 + 128, :], [], [xres], xres)
                    xres_of[k] = xres

                def a3(k):
                    job, ti, sub = subs3a[k]
                    if sub == 0:
                        CT = CTr.next()
                        dma("sp", CT[:, :, :], CATscr[job].rearrange("(c p) t -> p c t", p=128)[:, :, ti * 512:(ti + 1) * 512],
                            [], [CT], CT)
                        st3a["CT"] = CT
                    CT = st3a["CT"]
                    tok0 = ti * 512 + sub * 128
                    if k + 2 < len(subs3a):
                        load_xres(k + 2)
                    xres = xres_of.pop(k)
                    y1 = y1r.next()
                    pM = pMr.next()
                    for hf in range(2):
                        for c in range(8):
                            mm(pM[hf], pM[hf][:, :], CT, CT[:, c, sub * 128:(sub + 1) * 128],
                               Wo, Wo[:, c, hf * 512:(hf + 1) * 512], c == 0, c == 7)
                        stt("dve", y1, y1[:, hf * 512:(hf + 1) * 512], xres, xres[:, hf * 512:(hf + 1) * 512], ALPHA,
                            pM[hf], pM[hf][:, :], ALU.mult, ALU.add)
                    lnb = lnbr.next()
                    ln_a(lnb, y1)
                    st3a[k] = (y1, lnb)

                st3b = {}

                def b3a(k):
                    job, ti, sub = subs3a[k]
                    tok0 = ti * 512 + sub * 128
                    y1, lnb = st3a.pop(k)
                    x1 = x1r.next()
                    ln_b(lnb, y1, lngb1, x1, x1[:, :])
                    dma("sp", X1scr[job][tok0:tok0 + 128, :], x1[:, :], [x1], [], x1)
                    st3b[k] = x1

                def b3b(k):
                    job, ti, sub = subs3a[k]
                    tok0 = ti * 512 + sub * 128
                    x1 = st3b.pop(k)
                    cp("act", x1b, x1b[:, :], x1, x1[:, :])
                    for c in range(8):
                        S.op("pe", (lambda c: lambda e: e.transpose(out=pT[:, c, :], in_=x1b[:, c * 128:(c + 1) * 128],
                                                                    identity=identb[:, :]))(c),
                             reads=[x1b, identb], writes=[pT])
                    x1T = x1Tr.next()
                    cp("act", x1T, x1T[:, :, :], pT, pT[:, :, :])
                    dma("sp", X1Tscr[job].rearrange("(c p) t -> p c t", p=128)[:, :, tok0:tok0 + 128], x1T[:, :, :],
                        [x1T], [], x1T)

                n3 = len(subs3a)
                load_xres(0)
                if n3 > 1:
                    load_xres(1)
                a3(0)
                if n3 > 1:
                    a3(1)
                b3a(0)
                for k in range(n3):
                    if k + 2 < n3:
                        a3(k + 2)
                    if k + 1 < n3:
                        b3a(k + 1)
                    b3b(k)
                S.barrier()

            with ExitStack() as p3b:
                W1 = sbt(p3b, "W1", [128, 8, D_FF], BF16)
                lngb2 = sbt(p3b, "lngb2", [128, 2, D_MODEL], F32)
                dma("sp", lngb2[:, :, :], lngb_d[:, 2:4, :], [], [lngb2], lngb2)
                W2 = sbt(p3b, "W2", [128, 32, D_MODEL], BF16)
                w1v = w_ff1.rearrange("(c p) n -> p c n", p=128)
                w2v = w_ff2.rearrange("(c p) n -> p c n", p=128)
                for c in range(8):
                    dma("pool", W1[:, c, :], w1v[:, c, :], [], [W1], W1, nowaw=True)
                for c in range(32):
                    dma("pool", W2[:, c, :], w2v[:, c, :], [], [W2], W2, nowaw=True)
                X1Tr = ring(p3b, "X1T", 1, [128, 8, 512], BF16)
                HT = sbt(p3b, "HT", [128, 32, 512], BF16)
                rlr = ring(p3b, "rl", 2, [128, 512], F32)
                x1rr = ring(p3b, "x1res", 1, [128, D_MODEL], F32)
                y2r = ring(p3b, "y2", 2, [128, D_MODEL], F32)
                yor = ring(p3b, "yo", 2, [128, D_MODEL], F32)
                pH = Ring([pst(p3b, "pH%d" % i, [128, 512]) for i in range(3)])
                pO2 = [pst(p3b, "pO2%d" % i, [128, 512]) for i in range(2)]
                pend3b = []
                for job in ("p", "s"):
                    for ti in range(NT):
                        X1T = X1Tr.next()
                        dma("sp", X1T[:, :, :], X1Tscr[job].rearrange("(c p) t -> p c t", p=128)[:, :, ti * 512:(ti + 1) * 512],
                            [], [X1T], X1T)
                        for f in range(32):
                            ph = pH.next()
                            for c in range(8):
                                mm(ph, ph[:, :], W1, W1[:, c, f * 128:(f + 1) * 128], X1T, X1T[:, c, :], c == 0, c == 7)
                            rl = rlr.next()
                            act(rl, rl[:, :], ph, ph[:, :], AF.Relu)
                            tt("pool", HT, HT[:, f, :], rl, rl[:, :], rl, rl[:, :], ALU.mult)
                        for sub in range(4):
                            tok0 = ti * 512 + sub * 128
                            x1res = x1rr.next()
                            dma("sp", x1res[:, :], X1scr[job][tok0:tok0 + 128, :], [], [x1res], x1res)
                            y2 = y2r.next()
                            for hf in range(2):
                                for f in range(32):
                                    mm(pO2[hf], pO2[hf][:, :], HT, HT[:, f, sub * 128:(sub + 1) * 128],
                                       W2, W2[:, f, hf * 512:(hf + 1) * 512], f == 0, f == 31)
                                stt("dve", y2, y2[:, hf * 512:(hf + 1) * 512], x1res, x1res[:, hf * 512:(hf + 1) * 512], ALPHA,
                                    pO2[hf], pO2[hf][:, :], ALU.mult, ALU.add)
                            lnb = lnbr.next()
                            ln_a(lnb, y2)
                            if pend3b:
                                pend3b.pop(0)()

                            def fin(lnb=lnb, y2=y2, job=job, tok0=tok0):
                                yo = yor.next()
                                ln_b(lnb, y2, lngb2, yo, yo[:, :])
                                dma("sp", yout[job][tok0:tok0 + 128, :], yo[:, :], [yo], [], yo)
                            pend3b.append(fin)
                while pend3b:
                    pend3b.pop(0)()
                S.barrier()

        S.emit(newsem)
    return nc


def _consts(L):
    NT, NKO = L // 512, L // 128
    k = np.arange(128)
    jj, ii = np.meshgrid(k, k, indexing="ij")
    tri = np.zeros((128, 6, 128), np.float32)
    tri[:, 0] = np.where(jj <= ii, -1.0 / 16, 0.0)
    tri[:, 1] = np.where(jj > ii, -1.0 / 16, 0.0)
    tri[:, 2] = np.where(jj <= ii, 1.0, 0.0)
    tri[:, 3] = np.where(jj >= ii, -1.0 / 16, 0.0)
    tri[:, 4] = np.where(jj < ii, -1.0 / 16, 0.0)
    tri[:, 5] = np.where(jj >= ii, 1.0, 0.0)
    ident = np.eye(128, dtype=np.float32)
    q = np.arange(512)
    dtab = np.zeros((128, 4, 512), np.float32)
    for j in range(4):
        dtab[:, j, :] = np.abs(q[None, :] - (128 * j + k[:, None]))
    biasBA = np.zeros((128, 4, 2, NKO + 1), np.float32)
    jv = np.arange(NKO + 1)
    for h in range(4):
        biasBA[:, h, 0, :] = -SLOPES[h] * (128.0 * jv[None, :] - k[:, None])
        biasBA[:, h, 1, :] = -SLOPES[h] * (128.0 * jv[None, :] + k[:, None])
    t_hi = (q // 2) * 2
    t_lo = q % 2
    qaug = np.zeros((2, 4, 2, 512), np.float32)
    for h in range(4):
        qaug[0, h, 0] = -SLOPES[h] * t_hi
        qaug[1, h, 0] = -SLOPES[h] * t_lo
        qaug[0, h, 1] = SLOPES[h] * t_hi
        qaug[1, h, 1] = SLOPES[h] * t_lo
    return tri, ident, dtab, biasBA, qaug


def make_in_maps(L, x_prompt, x_sample, w_in, w_o, lam_q1, lam_k1, lam_q2, lam_k2, diff_norm_g,
                 gla_wa2_f, gla_ba_f, gla_wa2_b, gla_ba_b, gla_norm_g, ln1_g, ln1_b, w_ff1, w_ff2, ln2_g, ln2_b):
    NT, NKO = L // 512, L // 128
    f = np.float32
    w_in0 = np.ascontiguousarray(w_in[0], dtype=f)
    tri, ident, dtab, biasBA, qaug = _consts(L)
    lamv = np.ascontiguousarray(np.broadcast_to(
        np.stack([lam_q1[0], lam_k1[0], lam_q2[0], lam_k2[0]])[None], (128, 4, 64)), dtype=f)
    lngb = np.ascontiguousarray(np.broadcast_to(
        np.stack([ln1_g[0], ln1_b[0], ln2_g[0], ln2_b[0]])[None], (128, 4, D_MODEL)), dtype=f)
    wa2aug_f = np.concatenate([gla_wa2_f[0], gla_ba_f[0][None]], 0)
    wa2aug_b = np.concatenate([gla_wa2_b[0], gla_ba_b[0][None]], 0)
    wlr_f = w_in0[:, O_LRF:O_LRB]
    wlr_b = w_in0[:, O_LRB:D_IN]
    shared = {
        "w_in": w_in0, "w_o": np.ascontiguousarray(w_o[0], dtype=f),
        "w_ff1": np.ascontiguousarray(w_ff1[0], dtype=f), "w_ff2": np.ascontiguousarray(w_ff2[0], dtype=f),
        "lamv": lamv, "dng": np.ascontiguousarray(diff_norm_g[0].reshape(128, 1), dtype=f),
        "gng": np.ascontiguousarray(gla_norm_g[0].reshape(128, 1), dtype=f), "lngb": lngb,
        "tri": tri, "ident": ident, "dtab": dtab, "biasBA": biasBA, "qaug": qaug.astype(ml_dtypes.bfloat16),
        "kaugs": np.ones((2, L), ml_dtypes.bfloat16),
    }
    k = np.arange(128)
    in_maps = []
    for c in range(8):
        b, j = c // 4, c % 4
        before = list(range(j))
        after = list(range(3, j, -1))
        slots = [(i, True) for i in before] + [(i, False) for i in after]
        nb = len(before)
        xo = np.zeros((3, D_MODEL, L), f)
        kaugp = np.ones((2, 4 * L), f)
        biasO = np.zeros((128, 4, NT, 3 * NKO), f)
        wlr = np.zeros((5, D_MODEL, 16), f)
        wa2 = np.zeros((17, 5, 256), f)
        for s, (i, isb) in enumerate(slots):
            seg = x_prompt[b, i * L:(i + 1) * L]
            pos = np.arange(i * L, (i + 1) * L)
            if not isb:
                seg = seg[::-1]
                pos = pos[::-1]
                kaugp[:, s * L:(s + 1) * L] = -1.0
            xo[s] = seg.T
            wlr[s] = wlr_f if isb else wlr_b
            wa2[:, s] = wa2aug_f if isb else wa2aug_b
            posm = pos.reshape(NKO, 128)
            for qi in range(NT):
                t0 = j * L + qi * 512
                d = (t0 - posm) if isb else (posm - t0)
                for h in range(4):
                    biasO[:, h, qi, s * NKO:(s + 1) * NKO] = (-SLOPES[h] * d.astype(f)).T
        wlr[3], wlr[4] = wlr_f, wlr_b
        wa2[:, 3], wa2[:, 4] = wa2aug_f, wa2aug_b
        flags = np.zeros((128, 9), f)
        for s in range(3):
            flags[:, s] = 1.0 if s == nb - 1 else 0.0
            flags[:, 3 + s] = 1.0 if (s == 2 and nb < 3) else 0.0
            flags[:, 6 + s] = 0.0 if s == nb - 1 else 1.0
        own = x_prompt[b, j * L:(j + 1) * L]
        m = dict(shared)
        m.update({
            "xpT": np.ascontiguousarray(own.T, dtype=f), "xp": np.ascontiguousarray(own, dtype=f),
            "xsT": np.ascontiguousarray(x_sample[c].T, dtype=f), "xs": np.ascontiguousarray(x_sample[c], dtype=f),
            "xoT": xo, "wlr": wlr, "wa2aug": wa2, "biasO": biasO, "kaugp": kaugp.astype(ml_dtypes.bfloat16), "flags": flags,
        })
        in_maps.append(m)
    return in_maps


_NC_CACHE = {}


def run(L, **inputs):
    inputs = {k_: np.asarray(v) for k_, v in inputs.items()}
    if L not in _NC_CACHE:
        _NC_CACHE[L] = build(L)
    nc = _NC_CACHE[L]
    in_maps = make_in_maps(L, **inputs)
    res = run_bass_kernel_spmd(nc, in_maps, core_ids=list(range(8)))
    yp = np.zeros((2, 4 * L, D_MODEL), np.float32)
    ys = np.zeros((8, L, D_MODEL), np.float32)
    for c in range(8):
        b, j = c // 4, c % 4
        yp[b, j * L:(j + 1) * L] = res.results[c]["yp"]
        ys[c] = res.results[c]["ys"]
    return yp, ys


def kernel(**inputs):
    return run(FULL_L, **inputs)
```
